# Optimizing a Trainium2 kernel written in Bass

```python
import math
import jax, jax.numpy as jnp
from jax import lax
import numpy as np

D_MODEL = 2048
BATCH = 8
SEQ = 2048
DEPTH = 2
DEC_BATCH = 16
DEC_SEQ = 64
PAST_LEN = 2048

CHUNK = 64
Q_BLOCK = 128
MLA_HEADS = 6
MLA_Q_RANK = 512
MLA_KV_RANK = 256
MLA_NOPE = 128
MLA_ROPE = 64
MLA_V = 128
ROPE_THETA = 10000.0
DIFF_HEADS = 4
DIFF_QK = 64
DIFF_V = 2 * DIFF_QK
BAND_HEADS = 6
BAND_DIM = 128
BAND_PREV_CHUNKS = 8
BAND_REL_CLIP = 256
T5_BUCKETS = 32
T5_MAX_DIST = 128
D_FF = 5632
CONV_W = 3
LN_EPS = 1e-5
RMS_EPS = 1e-6

IN_SIZES = (MLA_Q_RANK, MLA_KV_RANK, MLA_ROPE,
            DIFF_HEADS * 2 * DIFF_QK, DIFF_HEADS * 2 * DIFF_QK, DIFF_HEADS * DIFF_V,
            BAND_HEADS * BAND_DIM, BAND_HEADS * BAND_DIM, BAND_HEADS * BAND_DIM)
IN_COLS = sum(IN_SIZES)
MIX_WIDTH = MLA_HEADS * MLA_V + DIFF_HEADS * DIFF_V + BAND_HEADS * BAND_DIM
DEEPNORM_ALPHA = (2 * DEPTH) ** 0.25
DEEPNORM_BETA = (8 * DEPTH) ** -0.25

kernel_name = 'hybrid_streaming_encoder_step'


def layer_norm(x, g, b):
    xf = x.astype(jnp.float32)
    mu = jnp.mean(xf, -1, keepdims=True)
    var = jnp.mean(jnp.square(xf - mu), -1, keepdims=True)
    return ((xf - mu) * lax.rsqrt(var + LN_EPS) * g + b).astype(x.dtype)


def rms_norm(x, g):
    xf = x.astype(jnp.float32)
    return (xf * lax.rsqrt(jnp.mean(xf * xf, -1, keepdims=True) + RMS_EPS) * g).astype(x.dtype)


def rope(x, pos):
    half = x.shape[-1] // 2
    inv = ROPE_THETA ** (-jnp.arange(half, dtype=jnp.float32) / half)
    ang = pos.astype(jnp.float32)[:, None] * inv
    shape = (pos.shape[0],) + (1,) * (x.ndim - 3) + (half,)
    cos = jnp.cos(ang).reshape(shape)
    sin = jnp.sin(ang).reshape(shape)
    xf = x.astype(jnp.float32)
    x1, x2 = xf[..., :half], xf[..., half:]
    return jnp.concatenate([x1 * cos - x2 * sin, x1 * sin + x2 * cos], -1).astype(x.dtype)


def chunk_causal_mask(q_pos, k_pos):
    return (q_pos[:, None] // CHUNK) >= (k_pos[None, :] // CHUNK)


def t5_bucket(rel):
    half = T5_BUCKETS // 2
    exact = half // 2
    n = jnp.abs(rel)
    nf = jnp.maximum(n, 1).astype(jnp.float32)
    large = exact + (jnp.log(nf / exact) / math.log(T5_MAX_DIST / exact) * (half - exact)).astype(jnp.int32)
    large = jnp.minimum(large, half - 1)
    return jnp.where(rel > 0, half, 0) + jnp.where(n < exact, n, large)


def over_query_blocks(fn, q_pos, *qs):
    lq = q_pos.shape[0]
    if lq <= Q_BLOCK or lq % Q_BLOCK:
        return fn(q_pos, *qs)
    nb = lq // Q_BLOCK
    qp = q_pos.reshape(nb, Q_BLOCK)
    qb = [jnp.moveaxis(q.reshape((q.shape[0], nb, Q_BLOCK) + q.shape[2:]), 1, 0) for q in qs]
    out = lax.map(lambda a: fn(*a), (qp, *qb))
    out = jnp.moveaxis(out, 0, 1)
    return out.reshape((out.shape[0], lq) + out.shape[3:])


def mla_mixer(c_q, c_kv_raw, k_rope_raw, pos, past_ckv, past_krope, q_norm_g, w_uq, kv_norm_g, w_ukv):
    B, L, _ = c_q.shape
    q = (rms_norm(c_q, q_norm_g) @ w_uq).reshape(B, L, MLA_HEADS, MLA_NOPE + MLA_ROPE)
    q_nope = q[..., :MLA_NOPE]
    q_rope = rope(q[..., MLA_NOPE:], pos)
    ckv = rms_norm(c_kv_raw, kv_norm_g)
    krope = rope(k_rope_raw, pos)
    if past_ckv is None:
        ckv_all, krope_all, k_pos = ckv, krope, pos
    else:
        ckv_all = jnp.concatenate([past_ckv, ckv], 1)
        krope_all = jnp.concatenate([past_krope, krope], 1)
        k_pos = jnp.concatenate([jnp.arange(past_ckv.shape[1], dtype=jnp.int32), pos])
    lk = ckv_all.shape[1]
    kv = (ckv_all @ w_ukv).reshape(B, lk, MLA_HEADS, MLA_NOPE + MLA_V)
    k_nope, v = kv[..., :MLA_NOPE], kv[..., MLA_NOPE:]
    scale = (MLA_NOPE + MLA_ROPE) ** -0.5

    def attend(qp, qn, qr):
        s = (jnp.einsum('bqhd,bkhd->bhqk', qn, k_nope)
             + jnp.einsum('bqhr,bkr->bhqk', qr, krope_all)).astype(jnp.float32) * scale
        s = jnp.where(chunk_causal_mask(qp, k_pos), s, -jnp.inf)
        p = jax.nn.softmax(s, -1).astype(v.dtype)
        return jnp.einsum('bhqk,bkhd->bqhd', p, v)

    o = over_query_blocks(attend, pos, q_nope, q_rope)
    return o.reshape(B, L, MLA_HEADS * MLA_V), ckv, krope


def diff_mixer(d_q, d_k, d_v, pos, past_k, past_v, t5_table, lq1, lk1, lq2, lk2, subln_g, layer_idx):
    B, L, _ = d_q.shape
    q = d_q.reshape(B, L, DIFF_HEADS, 2, DIFF_QK)
    k_rows = d_k.reshape(B, L, DIFF_HEADS, 2 * DIFF_QK)
    v_rows = d_v.reshape(B, L, DIFF_HEADS, DIFF_V)
    if past_k is None:
        k_all, v_all, k_pos = k_rows, v_rows, pos
    else:
        k_all = jnp.concatenate([past_k, k_rows], 1)
        v_all = jnp.concatenate([past_v, v_rows], 1)
        k_pos = jnp.concatenate([jnp.arange(past_k.shape[1], dtype=jnp.int32), pos])
    lk = k_all.shape[1]
    k_all2 = k_all.reshape(B, lk, DIFF_HEADS, 2, DIFF_QK)
    lam_init = 0.8 - 0.6 * math.exp(-0.3 * layer_idx)
    lam = (jnp.exp(jnp.sum(lq1.astype(jnp.float32) * lk1.astype(jnp.float32)))
           - jnp.exp(jnp.sum(lq2.astype(jnp.float32) * lk2.astype(jnp.float32))) + lam_init)
    scale = DIFF_QK ** -0.5

    def attend(qp, qq):
        bias = t5_table[t5_bucket(k_pos[None, :] - qp[:, None])]
        bias = jnp.transpose(bias, (2, 0, 1)).astype(jnp.float32)
        s = jnp.einsum('bqhcd,bkhcd->bchqk', qq, k_all2).astype(jnp.float32) * scale + bias
        s = jnp.where(chunk_causal_mask(qp, k_pos), s, -jnp.inf)
        p = jax.nn.softmax(s, -1)
        a = p[:, 0] - lam * p[:, 1]
        return jnp.einsum('bhqk,bkhd->bqhd', a.astype(v_all.dtype), v_all)

    o = over_query_blocks(attend, pos, q)
    o = rms_norm(o, subln_g) * (1.0 - lam_init)
    return o.reshape(B, L, DIFF_HEADS * DIFF_V), k_rows, v_rows


def band_attend(q, k, v, q_pos, k_pos, rel_table):
    rel = jnp.clip(q_pos[:, :, None] - k_pos[:, None, :], -BAND_REL_CLIP, BAND_REL_CLIP) + BAND_REL_CLIP
    bias = jnp.transpose(rel_table[:, rel], (1, 0, 2, 3)).astype(jnp.float32)
    qc = (q_pos // CHUNK)[:, :, None]
    kc = (k_pos // CHUNK)[:, None, :]
    mask = (k_pos[:, None, :] >= 0) & (kc <= qc) & (kc >= qc - BAND_PREV_CHUNKS)
    s = jnp.einsum('bnqhd,bnkhd->bnhqk', q, k).astype(jnp.float32) * (BAND_DIM ** -0.5) + bias[None]
    s = jnp.where(mask[:, None], s, -jnp.inf)
    p = jax.nn.softmax(s, -1).astype(v.dtype)
    return jnp.einsum('bnhqk,bnkhd->bnqhd', p, v)


def band_mixer(b_q, b_k, b_v, pos, past_k, past_v, rel_table):
    B, L, _ = b_q.shape
    q = b_q.reshape(B, L, BAND_HEADS, BAND_DIM)
    k = b_k.reshape(B, L, BAND_HEADS, BAND_DIM)
    v = b_v.reshape(B, L, BAND_HEADS, BAND_DIM)
    band_rows = BAND_PREV_CHUNKS * CHUNK
    if past_k is None:
        nc = L // CHUNK
        idx = jnp.arange(nc)[:, None] + jnp.arange(BAND_PREV_CHUNKS + 1)[None, :]

        def gather_band(t):
            tc = t.reshape(B, nc, CHUNK, BAND_HEADS, BAND_DIM)
            tp = jnp.pad(tc, ((0, 0), (BAND_PREV_CHUNKS, 0), (0, 0), (0, 0), (0, 0)))
            return tp[:, idx].reshape(B, nc, (BAND_PREV_CHUNKS + 1) * CHUNK, BAND_HEADS, BAND_DIM)

        k_pos = (((idx - BAND_PREV_CHUNKS) * CHUNK)[:, :, None]
                 + jnp.arange(CHUNK, dtype=jnp.int32)[None, None, :]).reshape(nc, -1)
        o = band_attend(q.reshape(B, nc, CHUNK, BAND_HEADS, BAND_DIM), gather_band(k), gather_band(v),
                        pos.reshape(nc, CHUNK), k_pos, rel_table)
        keep = min(band_rows, L)
        new_k, new_v = k[:, L - keep:], v[:, L - keep:]
    else:
        w = past_k.shape[1]
        k_all = jnp.concatenate([past_k, k], 1)[:, None]
        v_all = jnp.concatenate([past_v, v], 1)[:, None]
        k_pos = jnp.concatenate([pos[0] - w + jnp.arange(w, dtype=jnp.int32), pos])[None]
        o = band_attend(q[:, None], k_all, v_all, pos[None], k_pos, rel_table)
        new_k, new_v = k, v
    return o.reshape(B, L, BAND_HEADS * BAND_DIM), new_k, new_v


def conv_ffn(x, prev, w_gate, w_up, conv_w, conv_b, w_down):
    B, L, _ = x.shape
    g = x @ w_gate
    u = x @ w_up
    if prev is None:
        prev = jnp.zeros((B, CONV_W - 1, D_FF), g.dtype)
    gp = jnp.concatenate([prev, g], 1)
    gc = conv_b + conv_w[0] * gp[:, 0:L]
    for j in range(1, CONV_W):
        gc = gc + conv_w[j] * gp[:, j:j + L]
    h = jax.nn.silu(gc) * u
    return h @ w_down, gp[:, L:]


def trunk_layer(x, pos, past, prm, l):
    if past is None:
        past = (None,) * 7
    p_ckv, p_krope, p_dk, p_dv, p_bk, p_bv, p_conv = past
    h = x @ prm['w_in'][l]
    offs = np.cumsum(IN_SIZES)[:-1].tolist()
    c_q, c_kv, k_rope, d_q, d_k, d_v, b_q, b_k, b_v = jnp.split(h, offs, axis=-1)
    o_a, ckv, krope = mla_mixer(c_q, c_kv, k_rope, pos, p_ckv, p_krope, prm['mla_q_norm'][l],
                                prm['mla_w_uq'][l], prm['mla_kv_norm'][l], prm['mla_w_ukv'][l])
    o_b, dk, dv = diff_mixer(d_q, d_k, d_v, pos, p_dk, p_dv, prm['t5_table'], prm['diff_lq1'][l],
                             prm['diff_lk1'][l], prm['diff_lq2'][l], prm['diff_lk2'][l],
                             prm['diff_subln'][l], l)
    o_c, bk, bv = band_mixer(b_q, b_k, b_v, pos, p_bk, p_bv, prm['band_rel_table'][l])
    mix = jnp.concatenate([o_a, o_b, o_c], -1) @ prm['w_o'][l]
    x = layer_norm(DEEPNORM_ALPHA * x + mix, prm['ln1_g'][l], prm['ln1_b'][l])
    f, conv_state = conv_ffn(x, p_conv, prm['ffn_w_gate'][l], prm['ffn_w_up'][l],
                             prm['ffn_conv_w'][l], prm['ffn_conv_b'][l], prm['ffn_w_down'][l])
    x = layer_norm(DEEPNORM_ALPHA * x + f, prm['ln2_g'][l], prm['ln2_b'][l])
    return x, (ckv, krope, dk, dv, bk, bv, conv_state)


def setup_inputs(seed: int = 0) -> dict:
    key = jax.random.key(seed)
    ks = jax.random.split(key, 32)
    nrm = lambda k, shape, s: jax.random.normal(k, shape, jnp.float32) * s
    cw = min(BAND_PREV_CHUNKS * CHUNK, PAST_LEN)
    return {
        'x_prompt': nrm(ks[0], (BATCH, SEQ, D_MODEL), 1.0),
        'x_sample': nrm(ks[1], (DEC_BATCH, DEC_SEQ, D_MODEL), 1.0),
        'cache_mla_ckv': nrm(ks[2], (DEPTH, DEC_BATCH, PAST_LEN, MLA_KV_RANK), 1.0),
        'cache_mla_krope': nrm(ks[3], (DEPTH, DEC_BATCH, PAST_LEN, MLA_ROPE), 1.0),
        'cache_diff_k': nrm(ks[4], (DEPTH, DEC_BATCH, PAST_LEN, DIFF_HEADS, 2 * DIFF_QK), 1.0),
        'cache_diff_v': nrm(ks[5], (DEPTH, DEC_BATCH, PAST_LEN, DIFF_HEADS, DIFF_V), 1.0),
        'cache_band_k': nrm(ks[6], (DEPTH, DEC_BATCH, cw, BAND_HEADS, BAND_DIM), 1.0),
        'cache_band_v': nrm(ks[7], (DEPTH, DEC_BATCH, cw, BAND_HEADS, BAND_DIM), 1.0),
        'state_ffn_conv': nrm(ks[8], (DEPTH, DEC_BATCH, CONV_W - 1, D_FF), 1.0),
        't5_table': nrm(ks[9], (T5_BUCKETS, DIFF_HEADS), 0.5),
        'w_in': nrm(ks[10], (DEPTH, D_MODEL, IN_COLS), D_MODEL ** -0.5),
        'mla_q_norm': 1.0 + nrm(ks[11], (DEPTH, MLA_Q_RANK), 0.02),
        'mla_w_uq': nrm(ks[12], (DEPTH, MLA_Q_RANK, MLA_HEADS * (MLA_NOPE + MLA_ROPE)), MLA_Q_RANK ** -0.5),
        'mla_kv_norm': 1.0 + nrm(ks[13], (DEPTH, MLA_KV_RANK), 0.02),
        'mla_w_ukv': nrm(ks[14], (DEPTH, MLA_KV_RANK, MLA_HEADS * (MLA_NOPE + MLA_V)), MLA_KV_RANK ** -0.5),
        'diff_lq1': nrm(ks[15], (DEPTH, DIFF_QK), 0.1),
        'diff_lk1': nrm(ks[16], (DEPTH, DIFF_QK), 0.1),
        'diff_lq2': nrm(ks[17], (DEPTH, DIFF_QK), 0.1),
        'diff_lk2': nrm(ks[18], (DEPTH, DIFF_QK), 0.1),
        'diff_subln': 1.0 + nrm(ks[19], (DEPTH, DIFF_V), 0.02),
        'band_rel_table': nrm(ks[20], (DEPTH, BAND_HEADS, 2 * BAND_REL_CLIP + 1), 0.5),
        'w_o': nrm(ks[21], (DEPTH, MIX_WIDTH, D_MODEL), MIX_WIDTH ** -0.5 * DEEPNORM_BETA),
        'ln1_g': 1.0 + nrm(ks[22], (DEPTH, D_MODEL), 0.02),
        'ln1_b': nrm(ks[23], (DEPTH, D_MODEL), 0.02),
        'ffn_w_gate': nrm(ks[24], (DEPTH, D_MODEL, D_FF), D_MODEL ** -0.5),
        'ffn_w_up': nrm(ks[25], (DEPTH, D_MODEL, D_FF), D_MODEL ** -0.5 * DEEPNORM_BETA),
        'ffn_conv_w': nrm(ks[26], (DEPTH, CONV_W, D_FF), CONV_W ** -0.5),
        'ffn_conv_b': nrm(ks[27], (DEPTH, D_FF), 0.01),
        'ffn_w_down': nrm(ks[28], (DEPTH, D_FF, D_MODEL), D_FF ** -0.5 * DEEPNORM_BETA),
        'ln2_g': 1.0 + nrm(ks[29], (DEPTH, D_MODEL), 0.02),
        'ln2_b': nrm(ks[30], (DEPTH, D_MODEL), 0.02),
    }


def reference(x_prompt, x_sample, cache_mla_ckv, cache_mla_krope, cache_diff_k, cache_diff_v,
              cache_band_k, cache_band_v, state_ffn_conv, t5_table, w_in, mla_q_norm, mla_w_uq,
              mla_kv_norm, mla_w_ukv, diff_lq1, diff_lk1, diff_lq2, diff_lk2, diff_subln,
              band_rel_table, w_o, ln1_g, ln1_b, ffn_w_gate, ffn_w_up, ffn_conv_w, ffn_conv_b,
              ffn_w_down, ln2_g, ln2_b):
    prm = {'t5_table': t5_table, 'w_in': w_in, 'mla_q_norm': mla_q_norm, 'mla_w_uq': mla_w_uq,
           'mla_kv_norm': mla_kv_norm, 'mla_w_ukv': mla_w_ukv, 'diff_lq1': diff_lq1,
           'diff_lk1': diff_lk1, 'diff_lq2': diff_lq2, 'diff_lk2': diff_lk2,
           'diff_subln': diff_subln, 'band_rel_table': band_rel_table, 'w_o': w_o,
           'ln1_g': ln1_g, 'ln1_b': ln1_b, 'ffn_w_gate': ffn_w_gate, 'ffn_w_up': ffn_w_up,
           'ffn_conv_w': ffn_conv_w, 'ffn_conv_b': ffn_conv_b, 'ffn_w_down': ffn_w_down,
           'ln2_g': ln2_g, 'ln2_b': ln2_b}
    past_len = cache_mla_ckv.shape[2]
    pos_p = jnp.arange(x_prompt.shape[1], dtype=jnp.int32)
    pos_s = past_len + jnp.arange(x_sample.shape[1], dtype=jnp.int32)
    y_prompt, y_sample = x_prompt, x_sample
    states_p, states_s = [], []
    for l in range(DEPTH):
        y_prompt, st_p = trunk_layer(y_prompt, pos_p, None, prm, l)
        past = (cache_mla_ckv[l], cache_mla_krope[l], cache_diff_k[l], cache_diff_v[l],
                cache_band_k[l], cache_band_v[l], state_ffn_conv[l])
        y_sample, st_s = trunk_layer(y_sample, pos_s, past, prm, l)
        states_p.append(st_p)
        states_s.append(st_s)
    p_ckv, p_krope, p_dk, p_dv, p_bk, p_bv, p_conv = [jnp.stack(t) for t in zip(*states_p)]
    s_ckv, s_krope, s_dk, s_dv, s_bk, s_bv, s_conv = [jnp.stack(t) for t in zip(*states_s)]
    return (y_prompt, y_sample, p_ckv, p_krope, p_dk, p_dv, p_bk, p_bv, p_conv,
            s_ckv, s_krope, s_dk, s_dv, s_bk, s_bv, s_conv)
```

```python
import math
import numpy as np
import concourse.bass as bass
import concourse.mybir as mybir
from concourse.bass_utils import run_bass_kernel_spmd

F32 = mybir.dt.float32
BF = mybir.dt.bfloat16
AF = mybir.ActivationFunctionType
ALU = mybir.AluOpType

D = 2048
SEQ = 2048
DEPTH = 2
PAST = 2048
DSEQ = 64
NH_A, NH_B, NH_C = 6, 4, 6
DFF = 5632
NFF = DFF // 128
INC = 4672
ALPHA = (2 * DEPTH) ** 0.25
LN_EPS = 1e-5
RMS_EPS = 1e-6
O_CQ, O_CKV, O_KR, O_DQ, O_DK, O_DV, O_BQ, O_BK, O_BV = 0, 512, 768, 832, 1344, 1856, 2368, 3136, 3904
V_QG, V_KVG, V_SUB, V_L1G, V_L1B, V_L2G, V_L2B, V_CW, V_CB, V_END = 0, 4, 6, 7, 23, 39, 55, 71, 203, 247
IN_STARTS = [0, 256, 512, 832, 1088, 1344, 1600, 1856, 2112, 2368, 2624, 2880, 3136, 3392, 3648, 3904, 4160, 4416]
WTILES = {"w_in": (18, 16 * 256), "w_kr": (1, 16 * 128), "w_uqn": (1, 4 * 768), "w_uqr": (1, 4 * 768), "w_ukk": (1, 2 * 768),
          "w_ukv": (1, 2 * 768), "w_o": (8, 16 * 256), "w_gu": (44, 16 * 256), "w_d": (22, 2 * 2048)}


def _tile_of(key, kind, a, b, c):
    if kind == "row":
        return a // 2
    if key == "w_in":
        return IN_STARTS.index(b)
    if key in ("w_o", "w_gu"):
        return b // 256
    return 0


SUPERS = [0, 1, 2, 3, 4]
DBG = set()
NLAYERS = DEPTH


class Tok:
    __slots__ = ("sid", "sem", "val", "eng")

    def __init__(self, sid, sem, val, eng):
        self.sid, self.sem, self.val, self.eng = sid, sem, val, eng


class Buf:
    __slots__ = ("w", "r", "const", "excl")

    def __init__(self, const=False, excl=False):
        self.w = []
        self.r = []
        self.const = const
        self.excl = excl


class Eng:
    def __init__(self, C, name, h):
        self.C, self.name, self.h = C, name, h
        self.sem = None
        self.sid = None
        self.cnt = 0
        self.seen = {}
        self.pend = None

    def wait(self, tok):
        if tok is None or self.C.plan:
            return
        if tok.eng is self and self.name == "pe":
            return
        if tok.val is None:
            raise RuntimeError("wait on unresolved pending token (engine %s waits on %s)" % (self.name, tok.eng.name))
        if self.seen.get(tok.sid, 0) >= tok.val:
            return
        self.h.wait_ge(tok.sem, tok.val)
        self.seen[tok.sid] = tok.val

    def signal(self, ins):
        if self.sem is None or self.cnt >= 30000:
            self.sid, self.sem = self.C.newsem(self.name)
            self.cnt = 0
        self.cnt += 1
        ins.then_inc(self.sem, 1)
        tok = Tok(self.sid, self.sem, self.cnt, self)
        if self.pend is not None:
            self.pend.sid, self.pend.sem, self.pend.val = self.sid, self.sem, self.cnt
            self.pend = None
        return tok

    def pending(self):
        if self.pend is None:
            self.pend = Tok(None, None, None, self)
        return self.pend


class DmaQ:
    def __init__(self, C, eng, k):
        self.C, self.eng, self.k = C, eng, k
        self.sems = None
        self.i = 0


class Ctx:
    def __init__(self, nc, plan, wplan):
        self.nc = nc
        self.plan = plan
        self.wplan = wplan if wplan is not None else []
        self.wrec = []
        self.nsem = 0
        self.E = {n: Eng(self, n, h) for n, h in [("pe", nc.tensor), ("act", nc.scalar), ("dve", nc.vector),
                                                  ("pool", nc.gpsimd), ("sp", nc.sync)]}
        self.Q = {"sp": DmaQ(self, self.E["sp"], 16), "pool": DmaQ(self, self.E["pool"], 12)}
        self.dummy = Tok(0, None, 0, None)
        self.store_toks = []
        self.nops = 0

    def newsem(self, name):
        self.nsem += 1
        return self.nsem, self.nc.alloc_semaphore("s_%s_%d" % (name, self.nsem))

    def _deps(self, E, reads, writes):
        for b in reads:
            for t in b.w:
                E.wait(t)
            if b.excl:
                for t in b.r:
                    E.wait(t)
        for b in writes:
            for t in b.w:
                E.wait(t)
            for t in b.r:
                E.wait(t)

    def _post(self, tok, reads, writes, append):
        for b in reads:
            if not b.const and (not b.r or b.r[-1] is not tok):
                b.r.append(tok)
        for b in writes:
            if append:
                b.w.append(tok)
            else:
                b.w = [tok]
            b.r = []

    def op(self, en, emit, reads=(), writes=(), signal=True):
        self.nops += 1
        if self.plan:
            return self.dummy
        E = self.E[en]
        self._deps(E, reads, writes)
        ins = emit(E.h)
        tok = E.signal(ins) if signal else E.pending()
        self._post(tok, reads, writes, False)
        return tok

    def dma(self, qn, out, in_, reads=(), writes=(), append=False, store=False):
        self.nops += 1
        if self.plan:
            return self.dummy
        q = self.Q[qn]
        E = q.eng
        if q.sems is None:
            q.sems = [self.newsem("dq" + qn) for _ in range(q.k)]
        self._deps(E, reads, writes)
        k = q.i % q.k
        n = q.i // q.k + 1
        q.i += 1
        sid, sem = q.sems[k]
        if n > 1:
            E.wait(Tok(sid, sem, 16 * (n - 1), None))
        ins = E.h.dma_start(out=out, in_=in_)
        ins.then_inc(sem, 16)
        tok = Tok(sid, sem, 16 * n, None)
        self._post(tok, reads, writes, append)
        if store:
            self.store_toks.append(tok)
        return tok


class Ring:
    def __init__(self, items):
        self.items = items
        self.i = 0

    def next(self):
        it = self.items[self.i % len(self.items)]
        self.i += 1
        return it


def build(plan, wplan, supers, nlayers, stop=99):
    nc = bass.Bass("TRN2", target_bir_lowering=False)
    C = Ctx(nc, plan, wplan)
    op, dma = C.op, C.dma

    def din(name, shape, dt=F32):
        return nc.dram_tensor(name, list(shape), dt, kind="ExternalInput").ap()

    def dout(name, shape):
        return nc.dram_tensor(name, list(shape), F32, kind="ExternalOutput").ap()

    def dscr(name, shape, dt=BF):
        return nc.dram_tensor(name, list(shape), dt).ap()

    x_p = din("x_p", [SEQ, D])
    x_s = din("x_s", [128, D])
    c_ckv = din("c_ckv", [DEPTH, 2, PAST, 256])
    c_kr = din("c_kr", [DEPTH, 2, PAST, 64])
    c_dk = din("c_dk", [DEPTH, 2, PAST, 512])
    c_dv = din("c_dv", [DEPTH, 2, PAST, 512])
    c_bk = din("c_bk", [DEPTH, 2, 512, 768])
    c_bv = din("c_bv", [DEPTH, 2, 512, 768])
    c_conv = din("c_conv", [DEPTH, 2, 2, DFF])
    W32 = {k_: din(k_, [DEPTH, nt_, 128, x_]) for k_, (nt_, x_) in WTILES.items()}
    pvec_d = din("pvec", [128, DEPTH, V_END])
    lamv_d = din("lamv", [128, DEPTH, 4, 64])
    cos_d = din("cosT", [64, SEQ + DSEQ])
    sin_d = din("sinT", [64, SEQ + DSEQ])
    dbias_d = din("dbias", [128, NH_B, 2, 128])
    dfar_d = din("dfar", [128, NH_B])
    bbias_d = din("bbias", [DEPTH, NH_C, 128, 5, 128])

    y_p = dout("y_p", [SEQ, D])
    y_s = dout("y_s", [128, D])
    OUTP = {"ckv": dout("p_ckv", [DEPTH, SEQ, 256]), "kr": dout("p_krope", [DEPTH, SEQ, 64]),
            "dk": dout("p_dk", [DEPTH, SEQ, 512]), "dv": dout("p_dv", [DEPTH, SEQ, 512]),
            "bk": dout("p_bk", [DEPTH, 512, 768]), "bv": dout("p_bv", [DEPTH, 512, 768])}
    p_conv = dout("p_conv", [DEPTH, 2, DFF])
    OUTS = {"ckv": dout("s_ckv", [DEPTH, 128, 256]), "kr": dout("s_krope", [DEPTH, 128, 64]),
            "dk": dout("s_dk", [DEPTH, 128, 512]), "dv": dout("s_dv", [DEPTH, 128, 512]),
            "bk": dout("s_bk", [DEPTH, 128, 768]), "bv": dout("s_bv", [DEPTH, 128, 768])}
    s_conv = dout("s_conv", [DEPTH, 2, 2, DFF])

    WB = {k: dscr("b_" + k, v.shape) for k, v in W32.items()}
    WBb = {k: Buf() for k in W32}
    LK = [SEQ, PAST + DSEQ, PAST + DSEQ]
    LKB = [SEQ, 512 + DSEQ, 512 + DSEQ]
    SC = {}
    SCB = {}
    for l in range(DEPTH):
        for q in range(3):
            SC["akT", l, q] = dscr("akT_%d_%d" % (l, q), [NH_A, 128, LK[q]])
            SC["kr", l, q] = dscr("kr_%d_%d" % (l, q), [64, LK[q]])
            SC["av", l, q] = dscr("av_%d_%d" % (l, q), [LK[q], 768])
            SC["dkT", l, q] = dscr("dkT_%d_%d" % (l, q), [NH_B, 128, LK[q]])
            SC["dv", l, q] = dscr("dv_%d_%d" % (l, q), [LK[q], 512])
            SC["bkT", l, q] = dscr("bkT_%d_%d" % (l, q), [NH_C, 128, LKB[q]])
            SC["bv", l, q] = dscr("bv_%d_%d" % (l, q), [LKB[q], 768])
            for k in ("akT", "kr", "av", "dkT", "dv", "bkT", "bv"):
                SCB[k, l, q] = Buf()

    def sb(name, shape, dt):
        return nc.alloc_sbuf_tensor("sb_" + name, list(shape), dt)

    TMAX = 512
    xres = sb("xres", [128, 16, TMAX], F32)
    xres_b = [Buf() for _ in range(16)]
    xT = sb("xT", [128, 16, TMAX], BF)
    xT_b = [Buf() for _ in range(16)]
    oT = sb("oT", [128, 16, TMAX], BF)
    oT_b = [Buf() for _ in range(16)]
    qa = sb("qa", [128, 6, TMAX], BF)
    qa_b = [Buf() for _ in range(6)]
    qr = sb("qr", [64, 6, TMAX], BF)
    qr_b = [Buf() for _ in range(6)]
    cqT = sb("cqT", [128, 4, TMAX], BF)
    cqT_b = [Buf() for _ in range(4)]
    ckvT = sb("ckvT", [128, 2, TMAX], BF)
    ckvT_b = [Buf() for _ in range(2)]
    ckvf = sb("ckvf", [128, 2, TMAX], F32)
    ckvf_b = [Buf() for _ in range(2)]
    krf = sb("krf", [64, TMAX], F32)
    krf_b = Buf()
    NWS = 3
    wslots = Ring([(sb("wsl%d" % i, [128, 4096], BF), Buf()) for i in range(NWS)])
    hbr = Ring([(sb("hb%d" % i, [128, TMAX], F32), Buf()) for i in range(2)])
    stgr = Ring([(sb("stg%d" % i, [128, 1024], F32), Buf()) for i in range(2)])
    vst = sb("vst", [128, 4, 768], BF)
    vst_b = Buf()
    kbr = Ring([(sb("kb%d" % i, [128, PAST + DSEQ], BF), Buf()) for i in range(2)])
    krb = sb("krb", [64, PAST + DSEQ], BF)
    krb_b = Buf()
    vbr = Ring([(sb("vb%d" % i, [128, 17, 128], BF), Buf()) for i in range(2)])
    ptr = Ring([(sb("pt%d" % i, [128, TMAX], BF), Buf()) for i in range(6)])
    tsr = Ring([(sb("ts%d" % i, [128, 128], F32), Buf()) for i in range(3)])
    f32r = Ring([(sb("ft%d" % i, [128, TMAX], F32), Buf()) for i in range(6)])
    cosb = sb("cosb", [64, TMAX], F32)
    sinb = sb("sinb", [64, TMAX], F32)
    cs_b = Buf()
    nm0 = sb("nm0", [128, TMAX], F32)
    nm1 = sb("nm1", [128, TMAX], F32)
    nm0_b, nm1_b = Buf(), Buf()
    gpr = Ring([(sb("gp%d" % i, [128, TMAX + 8], F32), Buf()) for i in range(2)])
    hTr = Ring([(sb("hT%d" % i, [128, 2, TMAX], BF), Buf()) for i in range(2)])
    dbias = sb("dbias", [128, NH_B, 2, 128], F32)
    dfar = sb("dfar", [128, NH_B], F32)
    bbr = Ring([(sb("bb%d" % i, [128, 5, 128], F32), Buf()) for i in range(2)])
    pvec = sb("pvec", [128, DEPTH, V_END], F32)
    lamv = sb("lamv", [128, DEPTH, 4, 64], F32)
    lamt = sb("lamt", [128, 64], F32)
    lams = sb("lams", [128, 16], F32)
    agb = sb("agb", [128, DEPTH, 32], F32)
    ident = sb("ident", [128, 128], F32)
    onesD = sb("onesD", [128, 128], F32)
    onesB = sb("onesB", [128, 128], BF)
    ones512 = sb("ones512", [128, 128], F32)
    ones256 = sb("ones256", [128, 128], F32)
    ones128 = sb("ones128", [128, 128], F32)
    carry = sb("carry", [128, DEPTH, 2, 2, NFF], F32)
    carry_b = [Buf() for _ in range(DEPTH)]
    cvst = sb("cvst", [NFF, 2, 2, 128], F32)
    cvst_b = Buf()
    const_b = Buf(const=True)

    banks = [(nc.alloc_psum_tensor("ps%d" % i, [128, 512], F32), Buf(excl=True)) for i in range(8)]
    ps_short = Ring(banks[0:4])
    ps_long = Ring(banks[4:8])

    if "noconst" not in DBG:
        dma("sp", pvec[:], pvec_d[:, :, :], writes=[const_b], append=True)
        dma("sp", lamv[:], lamv_d[:, :, :, :], writes=[const_b], append=True)
        dma("sp", dbias[:], dbias_d[:, :, :, :], writes=[const_b], append=True)
        dma("sp", dfar[:], dfar_d[:, :], writes=[const_b], append=True)
    cb0 = Buf()
    op("pool", lambda h: h.memset(ident[:], 0.0), writes=[cb0])
    op("pool", lambda h: h.affine_select(out=ident[:], in_=ident[:], pattern=[[-1, 128]], compare_op=ALU.not_equal,
                                         fill=1.0, base=0, channel_multiplier=1), reads=[cb0], writes=[cb0])
    for t_, v_ in ((onesD, 1.0 / D), (onesB, 1.0), (ones512, 1.0 / 512), (ones256, 1.0 / 256), (ones128, 1.0 / 128)):
        op("pool", lambda h, t_=t_, v_=v_: h.memset(t_[:], v_), writes=[cb0])
    op("pool", lambda h: h.memset(carry[:], 0.0), writes=carry_b)
    const_b.w.extend(cb0.w)
    for l in ([] if "nolam" in DBG else range(DEPTH)):
        lam_init = 0.8 - 0.6 * math.exp(-0.3 * l)
        tb = Buf()
        for i in range(2):
            op("dve", lambda h, i=i, l=l: h.tensor_tensor(out=lamt[:, :], in0=lamv[:, l, 2 * i, :], in1=lamv[:, l, 2 * i + 1, :],
                                                           op=ALU.mult), reads=[const_b], writes=[tb])
            op("dve", lambda h, i=i, l=l: h.reduce_sum(out=lams[:, l * 8 + i:l * 8 + i + 1], in_=lamt[:, :],
                                                        axis=mybir.AxisListType.X), reads=[tb], writes=[tb])
            op("act", lambda h, i=i, l=l: h.activation(out=lams[:, l * 8 + 2 + i:l * 8 + 3 + i],
                                                        in_=lams[:, l * 8 + i:l * 8 + i + 1], func=AF.Exp),
               reads=[tb], writes=[tb])
        op("dve", lambda h, l=l, li=lam_init: h.scalar_tensor_tensor(
            out=lams[:, l * 8 + 4:l * 8 + 5], in0=lams[:, l * 8 + 3:l * 8 + 4], scalar=-li,
            in1=lams[:, l * 8 + 2:l * 8 + 3], op0=ALU.add, op1=ALU.subtract), reads=[tb], writes=[tb])
        op("dve", lambda h, l=l, li=lam_init: h.tensor_scalar(
            out=lams[:, l * 8 + 5:l * 8 + 6], in0=pvec[:, l, V_SUB:V_SUB + 1], scalar1=(1.0 - li), scalar2=None,
            op0=ALU.mult), reads=[const_b, tb], writes=[tb])
        op("dve", lambda h, l=l: h.tensor_scalar(out=agb[:, l, :], in0=pvec[:, l, V_L1G:V_L1G + 32], scalar1=ALPHA,
                                                 scalar2=None, op0=ALU.mult), reads=[const_b, tb], writes=[tb])
        const_b.w.extend(tb.w)

    def ps_view(bank, m, n):
        return bank[0:m, 0:n]

    wstate = {"issued": 0, "used": 0, "slots": {}, "cast_next": 0}
    castbuf = {}
    CAST_LA = 4

    def _cast_upto(idx):
        while wstate["cast_next"] <= min(idx, len(C.wplan) - 1):
            ent = C.wplan[wstate["cast_next"]]
            wstate["cast_next"] += 1
            if ent in castbuf:
                continue
            key, l, kind, a, b, c = ent
            cb = Buf()
            castbuf[ent] = cb
            ti = _tile_of(key, kind, a, b, c)
            dma("pool", WB[key][l, ti, :, :], W32[key][l, ti, :, :], writes=[cb])

    def _issue_w(idx):
        key, l, kind, a, b, c = C.wplan[idx]
        _cast_upto(idx + CAST_LA)
        WBb[key] = castbuf[C.wplan[idx]]
        t, bf = wslots.next()
        nk_, nc_ = (a, c) if kind == "col" else (b, c)
        assert nk_ * nc_ == WTILES[key][1], (key, nk_, nc_)
        view = t[:, 0:nk_ * nc_].rearrange("p (k c) -> p k c", k=nk_)
        dma("sp", t[:, 0:nk_ * nc_], WB[key][l, _tile_of(key, kind, a, b, c), :, :], reads=[WBb[key]], writes=[bf])
        wstate["slots"][idx] = (view, bf)

    def load_w(key, l, kind, a, b, c):
        idx = wstate["used"]
        wstate["used"] += 1
        if C.plan:
            C.wrec.append((key, l, kind, a, b, c))
            t, bf = wslots.items[0]
            k = a if kind == "col" else b
            return t[:, 0:k * c].rearrange("p (k c) -> p k c", k=k), bf
        assert C.wplan[idx] == (key, l, kind, a, b, c), (idx, C.wplan[idx], (key, l, kind, a, b, c))
        while wstate["issued"] < min(len(C.wplan), idx + NWS):
            _issue_w(wstate["issued"])
            wstate["issued"] += 1
        return wstate["slots"].pop(idx)

    def mm_acc(bank_b, out_ap, pairs, reads):
        n = len(pairs)
        tok = None
        for i, pr in enumerate(pairs):
            lt, rh = pr[0], pr[1]
            rd = list(reads) + (list(pr[2]) if len(pr) > 2 else [])
            tok = op("pe", lambda h, lt=lt, rh=rh, i=i: h.matmul(out_ap, lhsT=lt, rhs=rh, start=(i == 0), stop=(i == n - 1)),
                     reads=rd, writes=[bank_b], signal=(i == n - 1))
        return tok

    def transpose_to(bank, bank_b, out_ap, in_ap, npart, in_bufs, first=True, signal=True):
        return op("pe", lambda h: h.transpose(out=out_ap, in_=in_ap, identity=ident[0:npart, 0:npart]),
                  reads=[const_b] + in_bufs, writes=[bank_b], signal=signal)

    def rstd_from(ps_ap, psb, out_t, out_b, n, eps, T):
        op("act", lambda h: h.activation(out=out_t[0:n, 0:T], in_=ps_ap, func=AF.Sqrt, bias=eps, scale=1.0),
           reads=[psb], writes=[out_b])
        op("dve", lambda h: h.reciprocal(out=out_t[0:n, 0:T], in_=out_t[0:n, 0:T]), reads=[out_b], writes=[out_b])

    class RowStage:
        def __init__(self, NT, dests, width):
            self.NT, self.dests, self.width = NT, dests, width
            self.cur = None
            self.col0 = 0
            self.fill = 0

        def add(self, src_ap, npart, src_bufs):
            NT = self.NT
            if self.cur is None:
                self.cur = stgr.next()
                self.fill = 0
            st, stb = self.cur
            sv = st[:, 0:NT * 256].rearrange("p (t c) -> p t c", t=NT)
            bank, bb = ps_long.next()
            pv = bank[:, 0:NT * 128].rearrange("p (t c) -> p t c", t=NT)
            for tt in range(NT):
                transpose_to(bank, bb, pv[:, tt, 0:npart], src_ap[:, tt * 128:(tt + 1) * 128], npart, src_bufs,
                             signal=(tt == NT - 1))
            f = self.fill
            op("act", lambda h: h.copy(out=sv[:, :, f:f + npart], in_=pv[:, :, 0:npart]), reads=[bb], writes=[stb])
            self.fill += npart
            if self.fill >= 256 or self.col0 + self.fill >= self.width:
                self.flush()

        def flush(self):
            if self.cur is None or self.fill == 0:
                return
            NT = self.NT
            st, stb = self.cur
            sv = st[:, 0:NT * 256].rearrange("p (t c) -> p t c", t=NT)
            c0, f = self.col0, self.fill
            for dst, dbuf, is_out in self.dests:
                dv = dst.rearrange("(t p) c -> p t c", p=128)[:, :, c0:c0 + f]
                dma("pool", dv, sv[:, :, 0:f], reads=[stb], writes=([dbuf] if dbuf is not None else []), append=True,
                    store=is_out)
            self.col0 += f
            self.cur = None
            self.fill = 0

    def attn_scores_exp(h_ap_pairs, reads, nk, N, bias_fn, scale, mask_fn, bias_bufs=()):
        bank, bb = ps_short.next()
        mm_acc(bb, bank[0:nk, 0:N], h_ap_pairs, reads)
        pt, ptb = ptr.next()
        segs = bias_fn() if bias_fn is not None else [(0, N, None, None)]
        for (c0, ncs, bias_ap, far_ap) in segs:
            if bias_ap is not None:
                ts, tsb = tsr.next()
                op("dve", lambda h, c0=c0, ncs=ncs, bias_ap=bias_ap, ts=ts: h.scalar_tensor_tensor(
                    out=ts[0:nk, 0:ncs], in0=bank[0:nk, c0:c0 + ncs], scalar=scale, in1=bias_ap, op0=ALU.mult,
                    op1=ALU.add), reads=[bb, const_b] + list(bias_bufs), writes=[tsb])
                op("act", lambda h, c0=c0, ncs=ncs, ts=ts: h.activation(out=pt[0:nk, c0:c0 + ncs], in_=ts[0:nk, 0:ncs],
                                                                       func=AF.Exp), reads=[tsb], writes=[ptb])
            elif far_ap is not None:
                op("act", lambda h, c0=c0, ncs=ncs, far_ap=far_ap: h.activation(
                    out=pt[0:nk, c0:c0 + ncs], in_=bank[0:nk, c0:c0 + ncs], func=AF.Exp, bias=far_ap, scale=scale),
                   reads=[bb, const_b], writes=[ptb])
            else:
                op("act", lambda h, c0=c0, ncs=ncs: h.activation(out=pt[0:nk, c0:c0 + ncs], in_=bank[0:nk, c0:c0 + ncs],
                                                                func=AF.Exp, scale=scale), reads=[bb], writes=[ptb])
        for (p0, p1, c0, c1) in (mask_fn() if mask_fn is not None else []):
            op("pool", lambda h, p0=p0, p1=p1, c0=c0, c1=c1: h.memset(pt[p0:p1, c0:c1], 0.0), writes=[ptb])
        return pt, ptb

    class AU:
        __slots__ = ("pairs", "reads", "nk", "N", "bias_fn", "scale", "mask_fn", "bias_bufs", "acc", "oc0", "vt", "first", "last")

        def __init__(self, **kw):
            self.bias_fn = None
            self.mask_fn = None
            self.bias_bufs = ()
            for k_, v_ in kw.items():
                setattr(self, k_, v_)

    def attn_run(nheads, load_fn, units_fn, nacc, fin_fn, depth=3, late_lag=6):
        pipe = []
        late = []

        def tick_late(force=False):
            for it in list(late):
                it[0] -= 1
                if force or it[0] <= 0:
                    late.remove(it)
                    it[1]()

        cur = load_fn(0)
        for h_ in range(nheads):
            units = units_fn(h_, cur)
            accs = [ps_long.next() for _ in range(nacc)]
            nxt = None
            if len(units) <= depth:
                while pipe:
                    pipe.pop(0)()
            for ui, u in enumerate(units):
                pt, ptb = attn_scores_exp(u.pairs, u.reads, u.nk, u.N, u.bias_fn, u.scale, u.mask_fn, u.bias_bufs)

                def stage2(u=u, pt=pt, ptb=ptb, accs=accs, h_=h_, lastu=(ui == len(units) - 1), cur=cur):
                    vb, vbb = cur[2], cur[3]
                    ob, obb = accs[u.acc]
                    sb_, sbb = accs[u.acc + nacc // 2]
                    op("pe", lambda h: h.matmul(ob[:, u.oc0:u.oc0 + u.N], lhsT=vb[0:u.nk, u.vt, :], rhs=pt[0:u.nk, 0:u.N],
                                                start=u.first, stop=u.last), reads=[vbb, ptb], writes=[obb], signal=False)
                    op("pe", lambda h: h.matmul(sb_[:, u.oc0:u.oc0 + u.N], lhsT=onesB[0:u.nk, :], rhs=pt[0:u.nk, 0:u.N],
                                                start=u.first, stop=u.last), reads=[const_b, ptb], writes=[sbb], signal=True)
                    if lastu:
                        tick_late(force=True)
                        cont = fin_fn(h_, accs)
                        if cont is not None:
                            late.append([late_lag, cont])

                pipe.append(stage2)
                if len(pipe) > depth:
                    pipe.pop(0)()
                tick_late()
                if ui == min(depth, len(units) - 1) and h_ + 1 < nheads:
                    nxt = load_fn(h_ + 1)
            cur = nxt
        while pipe:
            pipe.pop(0)()
        tick_late(force=True)

    def softmax_fin(oacc, sacc, out_ap, out_b, ncs):
        (ob, obb), (sb_, sbb) = oacc, sacc
        fo, fob = f32r.next()
        fs, fsb = f32r.next()
        op("act", lambda h: h.copy(out=fo[:, 0:ncs], in_=ob[:, 0:ncs]), reads=[obb], writes=[fob])
        op("dve", lambda h: h.tensor_copy(out=fs[:, 0:ncs], in_=sb_[:, 0:ncs]), reads=[sbb], writes=[fsb])
        op("dve", lambda h: h.reciprocal(out=fs[:, 0:ncs], in_=fs[:, 0:ncs]), reads=[fsb], writes=[fsb])
        op("pool", lambda h: h.tensor_tensor(out=out_ap, in0=fo[:, 0:ncs], in1=fs[:, 0:ncs], op=ALU.mult), reads=[fob, fsb],
           writes=[out_b])

    def load_kv(kind_k, kind_v, l, q, h, k_lo, k_hi, vcol0):
        kb, kbb = kbr.next()
        vb, vbb = vbr.next()
        n = k_hi - k_lo
        dma("sp", kb[:, 0:n], SC[kind_k, l, q][h, :, k_lo:k_hi], reads=[SCB[kind_k, l, q]], writes=[kbb])
        nfull = n // 128
        src = SC[kind_v, l, q]
        if nfull:
            dma("sp", vb[:, 0:nfull, :], src[k_lo:k_lo + nfull * 128, vcol0:vcol0 + 128].rearrange("(t p) c -> p t c", p=128),
                reads=[SCB[kind_v, l, q]], writes=[vbb])
        rem = n - nfull * 128
        if rem:
            dma("sp", vb[0:rem, nfull, :], src[k_lo + nfull * 128:k_hi, vcol0:vcol0 + 128], reads=[SCB[kind_v, l, q]],
                writes=[vbb], append=True)
        return kb, kbb, vb, vbb

    def layer(s, l, T, segs, pos0):
        NT = T // 128
        is_p = (s < 4)
        t0 = s * 512 if is_p else 0
        OUT = OUTP if is_p else OUTS
        pv = lambda a, n=1: pvec[:, l, a:a + n]
        kq = lambda kind, kp: (512 if ((not is_p) and kind in ("bkT", "bv")) else kp)

        def out_rows(name):
            if is_p:
                return OUT[name][l, t0:t0 + T, :]
            return OUT[name][l, :, :]

        def scr_rows(kind, width):
            res = []
            for (q, c0, ncs, kp) in segs:
                res.append((SC[kind, l, q][kp:kp + ncs, :], SCB[kind, l, q]))
            return res

        def gemm_fm(key, col0, ncols, consumer, nk=16, rhs=None, rhs_b=None, wcol=256, lag=3):
            rhs = xT if rhs is None else rhs
            rhs_b = xT_b if rhs_b is None else rhs_b
            c = col0
            ci = 0
            pend = []
            while c < col0 + ncols:
                wc = min(wcol, col0 + ncols - c)
                wv, wb = load_w(key, l, "col", nk, c, wc)
                for j in range(0, wc, 128):
                    m = min(128, wc - j)
                    bank, bb = ps_short.next()
                    mm_acc(bb, bank[0:m, 0:T], [(wv[:, k, j:j + m], rhs[:, k, 0:T], [rhs_b[k]]) for k in range(nk)], [wb])
                    pend.append((ci, bank, bb, m))
                    if len(pend) > lag:
                        consumer(*pend.pop(0))
                    ci += 1
                c += wc
            while pend:
                consumer(*pend.pop(0))

        hT0, hT0b = hTr.items[0]
        hT1, hT1b = hTr.items[1]
        dq_t = lambda hh: (hT0 if hh < 2 else hT1)
        dq_b = lambda hh: (hT0b if hh < 2 else hT1b)

        def cons_dq(ci, bank, bb, m):
            op("act", lambda h: h.copy(out=dq_t(ci)[:, ci % 2, 0:T], in_=bank[:, 0:T]), reads=[bb], writes=[dq_b(ci)])

        def cons_q(ci, bank, bb, m):
            op("act", lambda h: h.copy(out=qa[:, ci, 0:T], in_=bank[:, 0:T]), reads=[bb], writes=[qa_b[ci]])

        def make_cons_k(kind, rs):
            def cons(ci, bank, bb, m):
                hb, hbb = hbr.next()
                op("act" if ci % 2 else "dve", lambda h: (h.copy(out=hb[:, 0:T], in_=bank[:, 0:T]) if ci % 2 else
                                                           h.tensor_copy(out=hb[:, 0:T], in_=bank[:, 0:T])), reads=[bb], writes=[hbb])
                for (q, c0, ncs, kp) in segs:
                    kp = kq(kind, kp)
                    dma("pool", SC[kind, l, q][ci, :, kp:kp + ncs], hb[:, c0:c0 + ncs], reads=[hbb], writes=[SCB[kind, l, q]],
                        append=True)
                if rs is not None:
                    rs.add(hb[:, 0:T], 128, [hbb])
            return cons

        def make_cons_v(rs):
            def cons(ci, bank, bb, m):
                hb, hbb = hbr.next()
                op("dve", lambda h: h.tensor_copy(out=hb[:, 0:T], in_=bank[:, 0:T]), reads=[bb], writes=[hbb])
                rs.add(hb[:, 0:T], 128, [hbb])
            return cons

        def vdests(kind, outname, width, want_out):
            d = []
            if want_out:
                d.append((out_rows(outname), None, True))
            return d

        class RowStageV(RowStage):
            def __init__(self, NT, dests, width, kind):
                RowStage.__init__(self, NT, dests, width)
                self.kind = kind

            def flush(self):
                if self.cur is None or self.fill == 0:
                    return
                st, stb = self.cur
                sv = st[:, 0:self.NT * 256].rearrange("p (t c) -> p t c", t=self.NT)
                cc, f = self.col0, self.fill
                for (q, c0, ncs, kp) in segs:
                    kp = kq(self.kind, kp)
                    if ncs >= 128:
                        dma("pool", SC[self.kind, l, q][kp:kp + ncs, cc:cc + f].rearrange("(t p) c -> p t c", p=128),
                            sv[:, :, 0:f], reads=[stb], writes=[SCB[self.kind, l, q]], append=True)
                    else:
                        dma("pool", SC[self.kind, l, q][kp:kp + ncs, cc:cc + f], sv[c0:c0 + ncs, 0, 0:f], reads=[stb],
                            writes=[SCB[self.kind, l, q]], append=True)
                RowStage.flush(self)

        for ii_, (q, c0, ncs, kp) in enumerate(segs):
            dma("sp", cosb[:, c0:c0 + ncs], cos_d[:, kp:kp + ncs], writes=[cs_b], append=(ii_ > 0))
            dma("sp", sinb[:, c0:c0 + ncs], sin_d[:, kp:kp + ncs], writes=[cs_b], append=True)

        ssq_bank, ssq_b = ps_long.next()
        sq_list = []

        def cons_cq(ci, bank, bb, m):
            ft, ftb = f32r.next()
            op("act", lambda h: h.activation(out=ft[:, 0:T], in_=bank[:, 0:T], func=AF.Square), reads=[bb], writes=[ftb])
            op("dve", lambda h: h.tensor_scalar(out=cqT[:, ci, 0:T], in0=bank[:, 0:T], scalar1=pv(V_QG + ci), scalar2=None,
                                                op0=ALU.mult), reads=[bb, const_b], writes=[cqT_b[ci]])
            op("pe", lambda h: h.matmul(ssq_bank[:, 0:T], lhsT=ones512[:, :], rhs=ft[:, 0:T], start=(ci == 0), stop=(ci == 3)),
               reads=[ftb, const_b], writes=[ssq_b], signal=True)

        gemm_fm("w_in", O_CQ, 512, cons_cq)
        rstd_from(ssq_bank[:, 0:T], ssq_b, nm0, nm0_b, 128, RMS_EPS, T)

        ssk_bank, ssk_b = ps_long.next()

        def cons_ckv(ci, bank, bb, m):
            ft, ftb = f32r.next()
            op("act", lambda h: h.activation(out=ft[:, 0:T], in_=bank[:, 0:T], func=AF.Square), reads=[bb], writes=[ftb])
            op("dve", lambda h: h.tensor_copy(out=ckvf[:, ci, 0:T], in_=bank[:, 0:T]), reads=[bb], writes=[ckvf_b[ci]])
            op("pe", lambda h: h.matmul(ssk_bank[:, 0:T], lhsT=ones256[:, :], rhs=ft[:, 0:T], start=(ci == 0), stop=(ci == 1)),
               reads=[ftb, const_b], writes=[ssk_b], signal=True)

        gemm_fm("w_in", O_CKV, 256, cons_ckv)
        rstd_from(ssk_bank[:, 0:T], ssk_b, nm1, nm1_b, 128, RMS_EPS, T)
        gemm_fm("w_in", O_DQ, 512, cons_dq)
        rs = RowStage(NT, [(out_rows("dk"), None, True)], 512)
        gemm_fm("w_in", O_DK, 512, make_cons_k("dkT", rs))
        rs.flush()
        rs = RowStageV(NT, [(out_rows("dv"), None, True)], 512, "dv")
        gemm_fm("w_in", O_DV, 512, make_cons_v(rs))
        rs.flush()
        rs_ckv = RowStage(NT, [(out_rows("ckv"), None, True)], 256)
        for ci in range(2):
            op("dve", lambda h, ci=ci: h.tensor_tensor(out=ckvf[:, ci, 0:T], in0=ckvf[:, ci, 0:T], in1=nm1[:, 0:T], op=ALU.mult),
               reads=[nm1_b], writes=[ckvf_b[ci]])
            op("dve", lambda h, ci=ci: h.tensor_scalar(out=ckvf[:, ci, 0:T], in0=ckvf[:, ci, 0:T], scalar1=pv(V_KVG + ci),
                                                       scalar2=None, op0=ALU.mult), reads=[const_b], writes=[ckvf_b[ci]])
            op("act", lambda h, ci=ci: h.copy(out=ckvT[:, ci, 0:T], in_=ckvf[:, ci, 0:T]), reads=[ckvf_b[ci]], writes=[ckvT_b[ci]])
            rs_ckv.add(ckvf[:, ci, 0:T], 128, [ckvf_b[ci]])
        rs_ckv.flush()

        if stop <= 3:
            return
        def cons_kn(ci, bank, bb, m):
            pt, ptb = ptr.next()
            op("act", lambda h: h.copy(out=pt[:, 0:T], in_=bank[:, 0:T]), reads=[bb], writes=[ptb])
            for (q, c0, ncs, kp) in segs:
                dma("pool", SC["akT", l, q][ci, :, kp:kp + ncs], pt[:, c0:c0 + ncs], reads=[ptb], writes=[SCB["akT", l, q]],
                    append=True)

        gemm_fm("w_ukk", 0, 768, cons_kn, nk=2, rhs=ckvT, rhs_b=ckvT_b, wcol=768)
        wv, wb = load_w("w_ukv", l, "col", 2, 0, 768)
        for tt in range(NT):
            for (cc0, cw) in ((0, 512), (512, 256)):
                bank, bb = ps_short.next()
                mm_acc(bb, bank[:, 0:cw], [(ckvT[:, k, tt * 128:(tt + 1) * 128], wv[:, k, cc0:cc0 + cw]) for k in range(2)],
                       [wb] + ckvT_b)
                op("act" if cc0 else "dve", lambda h, bank=bank, cc0=cc0, cw=cw, tt=tt: (
                    h.copy(out=vst[:, tt, cc0:cc0 + cw], in_=bank[:, 0:cw]) if cc0 else
                    h.tensor_copy(out=vst[:, tt, cc0:cc0 + cw], in_=bank[:, 0:cw])), reads=[bb], writes=[vst_b])
        for (q, c0, ncs, kp) in segs:
            if ncs >= 128:
                dma("pool", SC["av", l, q][kp:kp + ncs, :].rearrange("(t p) c -> p t c", p=128), vst[:, 0:NT, :], reads=[vst_b],
                    writes=[SCB["av", l, q]], append=True)
            else:
                dma("pool", SC["av", l, q][kp:kp + ncs, :], vst[c0:c0 + ncs, 0, :], reads=[vst_b], writes=[SCB["av", l, q]],
                    append=True)

        if stop <= 1:
            return
        def rope_from(bankA, bbA, bankB, bbB, out_ap, out_b, extra_scale=None):
            f1, f1b = f32r.next()
            f2, f2b = f32r.next()
            op("dve", lambda h: h.tensor_tensor(out=f1[0:64, 0:T], in0=bankA[0:64, 0:T], in1=cosb[:, 0:T], op=ALU.mult),
               reads=[bbA, cs_b], writes=[f1b])
            op("dve", lambda h: h.tensor_tensor(out=f2[0:64, 0:T], in0=bankB[0:64, 0:T], in1=sinb[:, 0:T], op=ALU.mult),
               reads=[bbB, cs_b], writes=[f2b])
            if extra_scale is None:
                op("dve", lambda h: h.tensor_tensor(out=out_ap, in0=f1[0:64, 0:T], in1=f2[0:64, 0:T], op=ALU.add),
                   reads=[f1b, f2b], writes=[out_b])
            else:
                op("dve", lambda h: h.tensor_tensor(out=f1[0:64, 0:T], in0=f1[0:64, 0:T], in1=f2[0:64, 0:T], op=ALU.add),
                   reads=[f2b], writes=[f1b])
                op("dve", lambda h: h.tensor_tensor(out=out_ap, in0=f1[0:64, 0:T], in1=extra_scale, op=ALU.mult),
                   reads=[f1b, nm0_b], writes=[out_b])

        wv, wb = load_w("w_kr", l, "col", 16, 0, 128)
        bA, bbA = ps_short.next()
        mm_acc(bbA, bA[0:64, 0:T], [(wv[:, k, 0:64], xT[:, k, 0:T]) for k in range(16)], [wb] + xT_b)
        bB, bbB = ps_short.next()
        mm_acc(bbB, bB[0:64, 0:T], [(wv[:, k, 64:128], xT[:, k, 0:T]) for k in range(16)], [wb] + xT_b)
        rope_from(bA, bbA, bB, bbB, krf[:, 0:T], krf_b)
        rs_kr = RowStage(NT, [(out_rows("kr"), None, True)], 64)
        rs_kr.add(krf[:, 0:T], 64, [krf_b])
        rs_kr.flush()
        pt, ptb = ptr.next()
        op("act", lambda h: h.copy(out=pt[0:64, 0:T], in_=krf[:, 0:T]), reads=[krf_b], writes=[ptb])
        for (q, c0, ncs, kp) in segs:
            dma("pool", SC["kr", l, q][:, kp:kp + ncs], pt[0:64, c0:c0 + ncs], reads=[ptb], writes=[SCB["kr", l, q]], append=True)

        if stop <= 2:
            return
        def cons_qn(ci, bank, bb, m):
            op("dve", lambda h: h.tensor_tensor(out=qa[:, ci, 0:T], in0=bank[:, 0:T], in1=nm0[:, 0:T], op=ALU.mult),
               reads=[bb, nm0_b], writes=[qa_b[ci]])

        gemm_fm("w_uqn", 0, 768, cons_qn, nk=4, rhs=cqT, rhs_b=cqT_b, wcol=768)
        wv, wb = load_w("w_uqr", l, "col", 4, 0, 768)
        for hh in range(NH_A):
            bA, bbA = ps_short.next()
            mm_acc(bbA, bA[0:64, 0:T], [(wv[:, k, hh * 128:hh * 128 + 64], cqT[:, k, 0:T]) for k in range(4)], [wb] + cqT_b)
            bB, bbB = ps_short.next()
            mm_acc(bbB, bB[0:64, 0:T], [(wv[:, k, hh * 128 + 64:hh * 128 + 128], cqT[:, k, 0:T]) for k in range(4)], [wb] + cqT_b)
            rope_from(bA, bbA, bB, bbB, qr[:, hh, 0:T], qr_b[hh], extra_scale=nm0[0:64, 0:T])

        if stop <= 4:
            return
        sc_a = 192.0 ** -0.5
        for (q, c0, ncs, kp) in segs:
            lk = kp + ncs
            dma("sp", krb[:, 0:lk], SC["kr", l, q][:, 0:lk], reads=[SCB["kr", l, q]], writes=[krb_b])
            nkt = (lk + 127) // 128

            def a_load(hh, q=q, lk=lk):
                return load_kv("akT", "av", l, q, hh, 0, lk, hh * 128)

            def a_units(hh, cur, c0=c0, ncs=ncs, kp=kp, lk=lk, nkt=nkt):
                kb, kbb = cur[0], cur[1]
                us = []
                for kt in range(nkt):
                    nk = min(128, lk - kt * 128)
                    jd = kt * 128 - kp
                    qlo = max(0, jd) if is_p else 0
                    N = ncs - qlo
                    mask = (lambda: [(64, 128, 0, 64)]) if (is_p and jd >= 0) else None
                    us.append(AU(pairs=[(kb[:, kt * 128:kt * 128 + nk], qa[:, hh, c0 + qlo:c0 + ncs]),
                                        (krb[:, kt * 128:kt * 128 + nk], qr[:, hh, c0 + qlo:c0 + ncs])],
                                 reads=[kbb, krb_b, qa_b[hh], qr_b[hh]], nk=nk, N=N, scale=sc_a, mask_fn=mask, acc=0, oc0=qlo, vt=kt,
                                 first=(kt == 0), last=(kt == nkt - 1)))
                return us

            def a_fin(hh, accs, c0=c0, ncs=ncs):
                softmax_fin(accs[0], accs[1], oT[:, hh, c0:c0 + ncs], oT_b[hh], ncs)

            attn_run(NH_A, a_load, a_units, 2, a_fin)

        if stop <= 5:
            return

        if stop <= 6:
            return
        sc_b = 64.0 ** -0.5
        neglam = lams[:, l * 8 + 4:l * 8 + 5]
        subsc = lams[:, l * 8 + 5:l * 8 + 6]
        for (q, c0, ncs, kp) in segs:
            lk = kp + ncs
            nkt = (lk + 127) // 128

            def b_load(hh, q=q, lk=lk):
                return load_kv("dkT", "dv", l, q, hh, 0, lk, hh * 128)

            def b_units(hh, cur, c0=c0, ncs=ncs, kp=kp, lk=lk, nkt=nkt):
                kb, kbb = cur[0], cur[1]
                us = []
                for kt in range(nkt):
                    nk = min(128, lk - kt * 128)
                    jd = kt * 128 - kp
                    qlo = max(0, jd) if is_p else 0
                    N = ncs - qlo

                    def bias_fn(qlo=qlo, N=N, kt=kt, hh=hh, nk=nk):
                        res = []
                        qpos0 = kp + qlo
                        c = 0
                        while c < N:
                            qp = qpos0 + c
                            w = min(128 - (qp % 128), N - c) if is_p else N
                            jj = (qp // 128) - kt
                            qoff = qp % 128
                            if jj <= 1:
                                res.append((c, w, dbias[0:nk, hh, jj, qoff:qoff + w], None))
                            else:
                                res.append((c, N - c, None, dfar[0:nk, hh:hh + 1]))
                                break
                            c += w
                        return res

                    mask = None
                    for cpn in range(2):
                        us.append(AU(pairs=[(kb[64 * cpn:64 * cpn + 64, kt * 128:kt * 128 + nk],
                                             dq_t(hh)[64 * cpn:64 * cpn + 64, hh % 2, c0 + qlo:c0 + ncs])],
                                     reads=[kbb, dq_b(hh)], nk=nk, N=N, bias_fn=bias_fn, scale=sc_b, mask_fn=mask, acc=cpn, oc0=qlo,
                                     vt=kt, first=(kt == 0), last=(kt == nkt - 1)))
                return us

            def b_fin(hh, accs, c0=c0, ncs=ncs):
                ev = []
                for cpn in range(2):
                    ob, obb = accs[cpn]
                    sb_, sbb = accs[2 + cpn]
                    fo, fob = f32r.next()
                    fs, fsb = f32r.next()
                    op("act", lambda h, fo=fo, ob=ob: h.copy(out=fo[:, 0:ncs], in_=ob[:, 0:ncs]), reads=[obb], writes=[fob])
                    op("dve", lambda h, fs=fs, sb_=sb_: h.tensor_copy(out=fs[:, 0:ncs], in_=sb_[:, 0:ncs]), reads=[sbb], writes=[fsb])
                    ev.append((fo, fob, fs, fsb))
                a = []
                for (fo, fob, fs, fsb) in ev:
                    op("dve", lambda h, fs=fs: h.reciprocal(out=fs[:, 0:ncs], in_=fs[:, 0:ncs]), reads=[fsb], writes=[fsb])
                    op("dve", lambda h, fo=fo, fs=fs: h.tensor_tensor(out=fo[:, 0:ncs], in0=fo[:, 0:ncs], in1=fs[:, 0:ncs], op=ALU.mult),
                       reads=[fsb], writes=[fob])
                    a.append((fo, fob))
                (a0, a0b), (a1, a1b) = a
                op("dve", lambda h: h.scalar_tensor_tensor(out=a0[:, 0:ncs], in0=a1[:, 0:ncs], scalar=neglam, in1=a0[:, 0:ncs],
                                                           op0=ALU.mult, op1=ALU.add), reads=[a1b, const_b], writes=[a0b])
                op("act", lambda h: h.activation(out=a1[:, 0:ncs], in_=a0[:, 0:ncs], func=AF.Square), reads=[a0b], writes=[a1b])
                return lambda: b_fin2(hh, a0, a0b, a1, a1b, c0, ncs)

            def b_fin2(hh, a0, a0b, a1, a1b, c0, ncs):
                vbank, vbb_ = ps_short.next()
                op("pe", lambda h: h.matmul(vbank[:, 0:ncs], lhsT=ones128[:, :], rhs=a1[:, 0:ncs], start=True, stop=True),
                   reads=[a1b, const_b], writes=[vbb_], signal=True)
                r2, r2b = f32r.next()
                rstd_from(vbank[:, 0:ncs], vbb_, r2, r2b, 128, RMS_EPS, ncs)
                op("dve", lambda h: h.tensor_tensor(out=a0[:, 0:ncs], in0=a0[:, 0:ncs], in1=r2[:, 0:ncs], op=ALU.mult),
                   reads=[r2b], writes=[a0b])
                op("dve", lambda h: h.tensor_scalar(out=oT[:, 6 + hh, c0:c0 + ncs], in0=a0[:, 0:ncs], scalar1=subsc, scalar2=None,
                                                    op0=ALU.mult), reads=[a0b, const_b], writes=[oT_b[6 + hh]])

            attn_run(NH_B, b_load, b_units, 4, b_fin)

        if stop <= 7:
            return
        gemm_fm("w_in", O_BQ, 768, cons_q)
        want_band_out = (s == 3) or (not is_p)

        def band_out_rows(name):
            return OUT[name][l, :, :]

        rs = RowStage(NT, [(band_out_rows("bk"), None, True)], 768) if want_band_out else None
        gemm_fm("w_in", O_BK, 768, make_cons_k("bkT", rs))
        if rs is not None:
            rs.flush()
        rs = RowStageV(NT, [(band_out_rows("bv"), None, True)] if want_band_out else [], 768, "bv")
        gemm_fm("w_in", O_BV, 768, make_cons_v(rs))
        rs.flush()

        if stop <= 8:
            return
        sc_c = 128.0 ** -0.5
        for (q, c0, ncs, kp) in segs:
            kp = kq("bkT", kp)
            k_lo = max(0, kp - 512) if is_p else 0
            lk = kp + ncs

            def c_load(hh, q=q, lk=lk, k_lo=k_lo):
                bbt, bbb = bbr.next()
                dma("sp", bbt[:], bbias_d[l, hh, :, :, :], writes=[bbb])
                return load_kv("bkT", "bv", l, q, hh, k_lo, lk, hh * 128) + (bbt, bbb)

            def c_units(hh, cur, c0=c0, ncs=ncs, kp=kp, lk=lk, k_lo=k_lo):
                kb, kbb, bbt, bbb = cur[0], cur[1], cur[4], cur[5]
                us = []
                nqt = (ncs + 127) // 128
                for qi in range(nqt):
                    nq = min(128, ncs - qi * 128)
                    qpos = kp + qi * 128
                    kts = [kt for kt in range(qpos // 128 - 4, qpos // 128 + 1) if kt * 128 >= k_lo]
                    for ii, kt in enumerate(kts):
                        j = qpos // 128 - kt
                        nk = min(128, lk - kt * 128)
                        ko = kt * 128 - k_lo

                        def bias_fn(j=j, nk=nk, nq=nq, bbt=bbt):
                            return [(0, nq, bbt[0:nk, j, 0:nq], None)]

                        mask = None
                        us.append(AU(pairs=[(kb[:, ko:ko + nk], qa[:, hh, c0 + qi * 128:c0 + qi * 128 + nq])], reads=[kbb, qa_b[hh]],
                                     nk=nk, N=nq, bias_fn=bias_fn, scale=sc_c, mask_fn=mask, bias_bufs=[bbb], acc=0, oc0=qi * 128,
                                     vt=ko // 128, first=(ii == 0), last=(ii == len(kts) - 1)))
                return us

            def c_fin(hh, accs, c0=c0, ncs=ncs):
                softmax_fin(accs[0], accs[1], oT[:, 10 + hh, c0:c0 + ncs], oT_b[10 + hh], ncs)

            attn_run(NH_C, c_load, c_units, 2, c_fin, depth=4)

        if stop <= 9:
            return
        lnq = []
        lnb = {"p": (4, 5)}

        def ln_stats(c):
            lnq.append(c)
            if len(lnq) > 2:
                ln_stats_now(lnq.pop(0))

        def ln_stats_now(c):
            (mb, mbb), (qb, qbb) = banks[lnb["p"][0]], banks[lnb["p"][1]]
            ft, ftb = f32r.next()
            op("act", lambda h: h.activation(out=ft[:, 0:T], in_=xres[:, c, 0:T], func=AF.Square), reads=[xres_b[c]], writes=[ftb])
            op("pe", lambda h: h.matmul(mb[:, 0:T], lhsT=onesD[:, :], rhs=xres[:, c, 0:T], start=(c == 0), stop=(c == 15)),
               reads=[xres_b[c], const_b], writes=[mbb], signal=(c == 15))
            op("pe", lambda h: h.matmul(qb[:, 0:T], lhsT=onesD[:, :], rhs=ft[:, 0:T], start=(c == 0), stop=(c == 15)),
               reads=[ftb, const_b], writes=[qbb], signal=True)

        def ln_finish(gcol, bcol, fp_g, fp_b, want_bf):
            (mb, mbb), (qb, qbb) = banks[lnb["p"][0]], banks[lnb["p"][1]]
            while lnq:
                ln_stats_now(lnq.pop(0))
            op("act", lambda h: h.copy(out=nm0[:, 0:T], in_=mb[:, 0:T]), reads=[mbb], writes=[nm0_b])
            ft, ftb = f32r.next()
            op("dve", lambda h: h.tensor_tensor(out=ft[:, 0:T], in0=nm0[:, 0:T], in1=nm0[:, 0:T], op=ALU.mult), reads=[nm0_b], writes=[ftb])
            op("dve", lambda h: h.tensor_tensor(out=ft[:, 0:T], in0=qb[:, 0:T], in1=ft[:, 0:T], op=ALU.subtract), reads=[qbb],
               writes=[ftb])
            op("act", lambda h: h.activation(out=nm1[:, 0:T], in_=ft[:, 0:T], func=AF.Sqrt, bias=LN_EPS, scale=1.0), reads=[ftb],
               writes=[nm1_b])
            op("dve", lambda h: h.reciprocal(out=nm1[:, 0:T], in_=nm1[:, 0:T]), reads=[nm1_b], writes=[nm1_b])
            for c in range(16):
                ft, ftb = f32r.next()
                op("dve", lambda h, c=c, ft=ft: h.tensor_tensor(out=ft[:, 0:T], in0=xres[:, c, 0:T], in1=nm0[:, 0:T], op=ALU.subtract),
                   reads=[xres_b[c], nm0_b], writes=[ftb])
                op("dve", lambda h, c=c, ft=ft: h.tensor_tensor(out=ft[:, 0:T], in0=ft[:, 0:T], in1=nm1[:, 0:T], op=ALU.mult),
                   reads=[nm1_b], writes=[ftb])
                if want_bf:
                    op("act", lambda h, c=c, ft=ft: h.activation(out=xT[:, c, 0:T], in_=ft[:, 0:T], func=AF.Identity,
                                                               scale=pv(gcol + c), bias=pv(bcol + c)),
                       reads=[ftb, const_b], writes=[xT_b[c]])
                op("pool", lambda h, c=c, ft=ft: h.tensor_scalar(out=xres[:, c, 0:T], in0=ft[:, 0:T], scalar1=fp_g(c), scalar2=fp_b(c),
                                                                op0=ALU.mult, op1=ALU.add), reads=[ftb, const_b], writes=[xres_b[c]])

        def cons_wo(ci, bank, bb, m):
            op("dve", lambda h: h.scalar_tensor_tensor(out=xres[:, ci, 0:T], in0=xres[:, ci, 0:T], scalar=ALPHA, in1=bank[:, 0:T],
                                                       op0=ALU.mult, op1=ALU.add), reads=[bb], writes=[xres_b[ci]])
            lnb["p"] = (4, 5)
            ln_stats(ci)

        gemm_fm("w_o", 0, D, cons_wo, nk=16, rhs=oT, rhs_b=oT_b)
        ln_finish(V_L1G, V_L1B, lambda c: agb[:, l, c:c + 1], lambda c: agb[:, l, 16 + c:17 + c], True)

        if stop <= 10:
            return
        nseg = len(segs)
        L = T // nseg
        ps_dn = ps_long

        def ffn_down(g2, hT, hTb):
            wd, wdb = load_w("w_d", l, "row", g2 * 2, 2, D)
            for c in range(16):
                db, dbb = ps_dn.next()
                mm_acc(dbb, db[:, 0:T], [(wd[:, jj, c * 128:(c + 1) * 128], hT[:, jj, 0:T]) for jj in range(2)], [wdb, hTb])
                op("dve", lambda h, c=c, db=db: h.tensor_tensor(out=xres[:, c, 0:T], in0=db[:, 0:T], in1=xres[:, c, 0:T], op=ALU.add),
                   reads=[dbb], writes=[xres_b[c]])
                if g2 == NFF // 2 - 1:
                    lnb["p"] = (0, 1)
                    ln_stats(c)

        def ffn_mm(j):
            wgu, wgub = load_w("w_gu", l, "col", 16, j * 256, 256)
            gb, gbb = ps_short.next()
            mm_acc(gbb, gb[:, 0:T], [(wgu[:, k, 0:128], xT[:, k, 0:T], [xT_b[k]]) for k in range(16)], [wgub])
            ub, ubb = ps_short.next()
            mm_acc(ubb, ub[:, 0:T], [(wgu[:, k, 128:256], xT[:, k, 0:T], [xT_b[k]]) for k in range(16)], [wgub])
            gp, gpb = gpr.next()
            gpv = gp[:, 0:nseg * (L + 2)].rearrange("p (s c) -> p s c", s=nseg)
            op("act", lambda h: h.copy(out=gpv[:, :, 2:2 + L], in_=gb[:, 0:T].rearrange("p (s c) -> p s c", s=nseg)),
               reads=[gbb], writes=[gpb])
            fu, fub = f32r.next()
            op("act", lambda h: h.copy(out=fu[:, 0:T], in_=ub[:, 0:T]), reads=[ubb], writes=[fub])
            op("pool", lambda h: h.tensor_copy(out=gpv[:, :, 0:2], in_=carry[:, l, 0:nseg, :, j]), reads=[carry_b[l]], writes=[gpb])
            op("pool", lambda h: h.tensor_copy(out=carry[:, l, 0:nseg, :, j], in_=gpv[:, :, L:L + 2]), reads=[gpb],
               writes=[carry_b[l]])
            return (j, gpv, gpb, fu, fub)

        def ffn_post(st_):
            j, gpv, gpb, fu, fub = st_
            hT, hTb = hTr.items[(j // 2) % 2]
            jj = j % 2
            f1, f1b = f32r.next()
            f1v = f1[:, 0:T].rearrange("p (s c) -> p s c", s=nseg)
            cw = lambda i: pvec[:, l, V_CW + j * 3 + i:V_CW + j * 3 + i + 1]
            op("act", lambda h: h.activation(out=f1v, in_=gpv[:, :, 2:2 + L], func=AF.Identity, scale=cw(2),
                                             bias=pvec[:, l, V_CB + j:V_CB + j + 1]), reads=[gpb, const_b], writes=[f1b])
            op("dve", lambda h: h.scalar_tensor_tensor(out=f1v, in0=gpv[:, :, 1:1 + L], scalar=cw(1), in1=f1v, op0=ALU.mult,
                                                       op1=ALU.add), reads=[gpb, const_b], writes=[f1b])
            op("dve", lambda h: h.scalar_tensor_tensor(out=f1v, in0=gpv[:, :, 0:L], scalar=cw(0), in1=f1v, op0=ALU.mult,
                                                       op1=ALU.add), reads=[gpb, const_b], writes=[f1b])
            op("act", lambda h: h.activation(out=f1[:, 0:T], in_=f1[:, 0:T], func=AF.Silu), reads=[f1b], writes=[f1b])
            op("dve", lambda h: h.tensor_tensor(out=hT[:, jj, 0:T], in0=fu[:, 0:T], in1=f1[:, 0:T], op=ALU.mult), reads=[fub, f1b],
               writes=[hTb])

        stq = {}
        for j in range(NFF):
            stq[j] = ffn_mm(j)
            if j % 2 == 1 and j >= 3:
                g_ = (j - 3) // 2
                ffn_down(g_, *hTr.items[g_ % 2])
            if j >= 1:
                ffn_post(stq.pop(j - 1))
        ffn_post(stq.pop(NFF - 1))
        ffn_down(NFF // 2 - 1, *hTr.items[(NFF // 2 - 1) % 2])

        if stop <= 11:
            return
        last_layer = (l == nlayers - 1)
        ln_finish(V_L2G, V_L2B, lambda c: pv(V_L2G + c), lambda c: pv(V_L2B + c), not last_layer)

        if s == 3 or not is_p:
            for si, (q, c0, ncs, kp) in enumerate(segs):
                bank, bb = ps_short.next()
                for t_ in range(2):
                    transpose_to(bank, bb, bank[0:NFF, t_ * 128:(t_ + 1) * 128], carry[:, l, si, t_, :], 128, [carry_b[l]],
                                 signal=(t_ == 1))
                op("act", lambda h, si=si, bank=bank: h.copy(out=cvst[:, si, :, :],
                                                            in_=bank[0:NFF, 0:256].rearrange("p (t c) -> p t c", t=2)),
                   reads=[bb], writes=[cvst_b])
                dst = p_conv[l] if is_p else s_conv[l, si]
                dma("pool", dst.rearrange("t (j p) -> j t p", p=128), cvst[:, si, :, :], reads=[cvst_b], store=True)


    def prefetched(items, load, process, depth=5):
        q_ = []
        nxt_i = 0
        for i_, it in enumerate(items):
            while nxt_i < len(items) and nxt_i <= i_ + depth:
                q_.append(load(items[nxt_i]))
                nxt_i += 1
            process(it, q_.pop(0))

    def prep_caches():
        T = 512
        pslots = []
        for k_ in range(8):
            pb_ = Buf()
            for c_ in (2 * k_, 2 * k_ + 1):
                pb_.w.extend(xres_b[c_].w)
                pb_.r.extend(xres_b[c_].r)
            pslots.append((xres[:, 2 * k_:2 * k_ + 2, :].rearrange("p a b -> p (a b)"), pb_))
        stgr = Ring(pslots)
        vst_pb = [Buf() for _ in range(8)]
        for b_ in vst_pb:
            b_.w.extend(vst_b.w)
            b_.r.extend(vst_b.r)
        for l in range(nlayers):
            for si in range(2):
                q = 1 + si

                def a_ld(it):
                    blk, tt = it
                    r0 = blk * 512 + tt * 128
                    st, stb = stgr.next()
                    dma("sp", st[:, 0:256], c_ckv[l, si, r0:r0 + 128, :], writes=[stb])
                    dma("sp", st[:, 256:320], c_kr[l, si, r0:r0 + 128, :], writes=[stb], append=True)
                    return st, stb

                def a_pr(it, cur):
                    blk, tt = it
                    st, stb = cur
                    bank, bb = ps_short.next()
                    for ci in range(2):
                        transpose_to(bank, bb, bank[:, ci * 128:(ci + 1) * 128], st[:, ci * 128:(ci + 1) * 128], 128, [stb],
                                     signal=False)
                    transpose_to(bank, bb, bank[0:64, 256:384], st[:, 256:320], 128, [stb], signal=True)
                    op("act", lambda h: h.copy(out=ckvT[:, :, tt * 128:(tt + 1) * 128],
                                               in_=bank[:, 0:256].rearrange("p (c t) -> p c t", c=2)), reads=[bb], writes=ckvT_b)
                    op("dve", lambda h: h.tensor_copy(out=qr[:, 0, tt * 128:(tt + 1) * 128], in_=bank[0:64, 256:384]),
                       reads=[bb], writes=[qr_b[0]])
                    if tt < 3:
                        return
                    dma("pool", SC["kr", l, q][:, blk * 512:(blk + 1) * 512], qr[:, 0, 0:512], reads=[qr_b[0]],
                        writes=[SCB["kr", l, q]], append=True)
                    wv, wb = load_w("w_ukk", l, "col", 2, 0, 768)
                    for hh in range(NH_A):
                        bank, bb = ps_short.next()
                        mm_acc(bb, bank[:, 0:T], [(wv[:, k, hh * 128:(hh + 1) * 128], ckvT[:, k, 0:T]) for k in range(2)], [wb] + ckvT_b)
                        pt, ptb = ptr.next()
                        op("act", lambda h, pt=pt, bank=bank: h.copy(out=pt[:, 0:T], in_=bank[:, 0:T]), reads=[bb], writes=[ptb])
                        dma("pool", SC["akT", l, q][hh, :, blk * 512:(blk + 1) * 512], pt[:, 0:T], reads=[ptb],
                            writes=[SCB["akT", l, q]], append=True)
                    wv, wb = load_w("w_ukv", l, "col", 2, 0, 768)
                    for t2 in range(4):
                        for (cc0, cw) in ((0, 512), (512, 256)):
                            bank, bb = ps_short.next()
                            mm_acc(bb, bank[:, 0:cw], [(ckvT[:, k, t2 * 128:(t2 + 1) * 128], wv[:, k, cc0:cc0 + cw]) for k in range(2)],
                                   [wb] + ckvT_b)
                            vb_ = vst_pb[t2 * 2 + (1 if cc0 else 0)]
                            op("act" if cc0 else "dve", lambda h, bank=bank, cc0=cc0, cw=cw, t2=t2: (
                                h.copy(out=vst[:, t2, cc0:cc0 + cw], in_=bank[:, 0:cw]) if cc0 else
                                h.tensor_copy(out=vst[:, t2, cc0:cc0 + cw], in_=bank[:, 0:cw])), reads=[bb], writes=[vb_])
                    dma("pool", SC["av", l, q][blk * 512:(blk + 1) * 512, :].rearrange("(t p) c -> p t c", p=128), vst[:, 0:4, :],
                        reads=vst_pb, writes=[SCB["av", l, q]], append=True)

                prefetched([(blk, tt) for blk in range(PAST // 512) for tt in range(4)], a_ld, a_pr)

                def k_ld(it):
                    kind, blk, hh = it
                    st, stb = stgr.next()
                    sv = st[:, 0:512].rearrange("p (t c) -> p t c", t=4)
                    src = c_dk if kind == "dkT" else c_bk
                    dma("sp", sv, src[l, si, blk * 512:(blk + 1) * 512, hh * 128:(hh + 1) * 128].rearrange("(t p) c -> p t c", p=128),
                        writes=[stb])
                    return sv, stb

                def k_pr(it, cur):
                    kind, blk, hh = it
                    sv, stb = cur
                    bank, bb = ps_short.next()
                    for tt in range(4):
                        transpose_to(bank, bb, bank[:, tt * 128:(tt + 1) * 128], sv[:, tt, :], 128, [stb], signal=(tt == 3))
                    pt, ptb = ptr.next()
                    op("act" if hh % 2 else "dve", lambda h: (
                        h.copy(out=pt[:, 0:512], in_=bank[:, 0:512]) if hh % 2 else h.tensor_copy(out=pt[:, 0:512], in_=bank[:, 0:512])),
                       reads=[bb], writes=[ptb])
                    dma("pool", SC[kind, l, q][hh, :, blk * 512:(blk + 1) * 512], pt[:, 0:512], reads=[ptb],
                        writes=[SCB[kind, l, q]], append=True)

                for r0 in range(0, PAST, 1024):
                    dma("pool", SC["dv", l, q][r0:r0 + 1024, :], c_dv[l, si, r0:r0 + 1024, :], writes=[SCB["dv", l, q]], append=True)
                dma("pool", SC["bv", l, q][0:512, :], c_bv[l, si, :, :], writes=[SCB["bv", l, q]], append=True)
                prefetched([("dkT", blk, hh) for blk in range(PAST // 512) for hh in range(NH_B)] +
                           [("bkT", 0, hh) for hh in range(NH_C)], k_ld, k_pr)
                dma("sp", cvst[:, si, :, :], c_conv[l, si].rearrange("t (j p) -> j t p", p=128), writes=[cvst_b])
                bank, bb = ps_short.next()
                for t_ in range(2):
                    transpose_to(bank, bb, bank[:, t_ * NFF:(t_ + 1) * NFF], cvst[:, si, t_, :], NFF, [cvst_b], signal=(t_ == 1))
                op("act", lambda h, bank=bank, si=si, l=l: h.copy(out=carry[:, l, si, :, :],
                                                                 in_=bank[:, 0:2 * NFF].rearrange("p (t j) -> p t j", t=2)),
                   reads=[bb], writes=[carry_b[l]])
        for b_ in vst_pb:
            vst_b.r = list(vst_b.r) + list(b_.r)
            vst_b.w = list(vst_b.w) + list(b_.w)
        for k_, (_, pb_) in enumerate(pslots):
            for c_ in (2 * k_, 2 * k_ + 1):
                xres_b[c_].r = list(xres_b[c_].r) + list(pb_.r)
                xres_b[c_].w = list(xres_b[c_].w) + list(pb_.w)

    for s in supers:
        is_p = s < 4
        T = 512 if is_p else 128
        NT = T // 128
        xin = x_p[s * 512:(s + 1) * 512, :] if is_p else x_s
        if not is_p:
            prep_caches()
        for tt in ([] if "nox" in DBG else range(NT)):
            for hf in range(2):
                st, stb = stgr.next()
                dma("sp", st[:, 0:1024], xin[tt * 128:(tt + 1) * 128, hf * 1024:(hf + 1) * 1024], writes=[stb])
                for g4 in range(2):
                    bank, bb = ps_short.next()
                    for i in range(4):
                        cc = g4 * 4 + i
                        transpose_to(bank, bb, bank[:, i * 128:(i + 1) * 128], st[:, cc * 128:(cc + 1) * 128], 128, [stb],
                                     signal=(i == 3))
                    c0 = hf * 8 + g4 * 4
                    if "x_nodve" not in DBG:
                        op("dve", lambda h, bank=bank, c0=c0, tt=tt: h.tensor_copy(
                            out=xres[:, c0:c0 + 4, tt * 128:(tt + 1) * 128], in_=bank[:, 0:512].rearrange("p (c t) -> p c t", c=4)),
                           reads=[bb], writes=xres_b[c0:c0 + 4])
                    if "x_noact" not in DBG:
                        op("act", lambda h, bank=bank, c0=c0, tt=tt: h.copy(
                            out=xT[:, c0:c0 + 4, tt * 128:(tt + 1) * 128], in_=bank[:, 0:512].rearrange("p (c t) -> p c t", c=4)),
                           reads=[bb], writes=xT_b[c0:c0 + 4])
        if is_p:
            segs = [(0, 0, 512, s * 512)]
            pos0 = s * 512
        else:
            segs = [(1, 0, 64, PAST), (2, 64, 64, PAST)]
            pos0 = None
        for l in range(nlayers):
            layer(s, l, T, segs, pos0 if is_p else PAST)
        yout = y_p[s * 512:(s + 1) * 512, :] if is_p else y_s
        for tt in ([] if "noy" in DBG else range(NT)):
            for hf in range(2):
                st, stb = stgr.next()
                for g4 in range(2):
                    bank, bb = ps_short.next()
                    for i in range(4):
                        cc = hf * 8 + g4 * 4 + i
                        transpose_to(bank, bb, bank[:, i * 128:(i + 1) * 128], xres[:, cc, tt * 128:(tt + 1) * 128], 128,
                                     [xres_b[cc]], signal=(i == 3))
                    op("act" if g4 else "dve", lambda h, bank=bank, st=st, g4=g4: (
                        h.copy(out=st[:, g4 * 512:(g4 + 1) * 512], in_=bank[:, 0:512]) if g4 else
                        h.tensor_copy(out=st[:, g4 * 512:(g4 + 1) * 512], in_=bank[:, 0:512])), reads=[bb], writes=[stb])
                dma("pool", yout[tt * 128:(tt + 1) * 128, hf * 1024:(hf + 1) * 1024], st[:, 0:1024], reads=[stb], store=True)

    if not plan:
        E = C.E["pool"]
        for tok in C.store_toks:
            E.wait(tok)
        for qn in ("pool", "sp"):
            q = C.Q[qn]
            if q.sems is not None:
                for k in range(q.k):
                    n = (q.i - 1 - k) // q.k + 1 if q.i > k else 0
                    if n > 0:
                        sid, sem = q.sems[k]
                        E.wait(Tok(sid, sem, 16 * n, None))
    return nc, C


def _t5_bucket(rel):
    half, exact = 16, 8
    n = np.abs(rel)
    nf = np.maximum(n, 1).astype(np.float32)
    large = exact + (np.log(nf / exact) / np.float32(math.log(128 / exact)) * (half - exact)).astype(np.int32)
    large = np.minimum(large, half - 1)
    return np.where(rel > 0, half, 0) + np.where(n < exact, n, large)


_CACHE = {}


def _host_prep(x_prompt, x_sample, cache_mla_ckv, cache_mla_krope, cache_diff_k, cache_diff_v, cache_band_k, cache_band_v,
           state_ffn_conv, t5_table, w_in, mla_q_norm, mla_w_uq, mla_kv_norm, mla_w_ukv, diff_lq1, diff_lk1, diff_lq2,
           diff_lk2, diff_subln, band_rel_table, w_o, ln1_g, ln1_b, ffn_w_gate, ffn_w_up, ffn_conv_w, ffn_conv_b,
           ffn_w_down, ln2_g, ln2_b):
    f = lambda a: np.ascontiguousarray(np.asarray(a, dtype=np.float32))
    x_prompt, x_sample = f(x_prompt), f(x_sample)
    w_in = f(w_in)
    mla_w_uq = f(mla_w_uq)
    mla_w_ukv = f(mla_w_ukv)
    w_kr = np.concatenate([w_in[:, :, 768:832], w_in[:, :, 800:832], w_in[:, :, 768:800]], axis=2)
    uq = mla_w_uq.reshape(DEPTH, 512, NH_A, 192)
    w_uqn = uq[:, :, :, 0:128].reshape(DEPTH, 512, 768)
    w_uqr = np.concatenate([uq[:, :, :, 128:192], uq[:, :, :, 160:192], uq[:, :, :, 128:160]], axis=3).reshape(DEPTH, 512, 768)
    ukv = mla_w_ukv.reshape(DEPTH, 256, NH_A, 256)
    w_ukk = ukv[:, :, :, 0:128].reshape(DEPTH, 256, 768)
    w_ukv = ukv[:, :, :, 128:256].reshape(DEPTH, 256, 768)
    w_gu = np.ascontiguousarray(np.stack([f(ffn_w_gate).reshape(DEPTH, D, NFF, 128), f(ffn_w_up).reshape(DEPTH, D, NFF, 128)],
                                         axis=3).reshape(DEPTH, D, 2 * DFF))
    pvec = np.zeros((128, DEPTH, V_END), np.float32)
    cm = lambda v: f(v).reshape(-1, 128).T
    for l in range(DEPTH):
        pvec[:, l, V_QG:V_QG + 4] = cm(mla_q_norm[l])
        pvec[:, l, V_KVG:V_KVG + 2] = cm(mla_kv_norm[l])
        pvec[:, l, V_SUB:V_SUB + 1] = cm(diff_subln[l])
        pvec[:, l, V_L1G:V_L1G + 16] = cm(ln1_g[l])
        pvec[:, l, V_L1B:V_L1B + 16] = cm(ln1_b[l])
        pvec[:, l, V_L2G:V_L2G + 16] = cm(ln2_g[l])
        pvec[:, l, V_L2B:V_L2B + 16] = cm(ln2_b[l])
        cw = f(ffn_conv_w[l])
        pvec[:, l, V_CW:V_CW + 132] = cw.reshape(3, NFF, 128).transpose(2, 1, 0).reshape(128, 132)
        pvec[:, l, V_CB:V_CB + 44] = cm(ffn_conv_b[l])
    lamv = np.zeros((128, DEPTH, 4, 64), np.float32)
    for l in range(DEPTH):
        for i, v in enumerate((diff_lq1, diff_lk1, diff_lq2, diff_lk2)):
            lamv[:, l, i, :] = f(v)[l][None, :]
    half = 32
    inv = (10000.0 ** (-np.arange(half, dtype=np.float32) / half)).astype(np.float32)
    pos = np.arange(SEQ + DSEQ, dtype=np.float32)
    ang = pos[:, None] * inv[None, :]
    cosT = np.concatenate([np.cos(ang), np.cos(ang)], 1).T.astype(np.float32)
    sinT = np.concatenate([-np.sin(ang), np.sin(ang)], 1).T.astype(np.float32)
    t5 = f(t5_table)
    kp = np.arange(128)[:, None]
    qq = np.arange(128)[None, :]
    dbias = np.zeros((128, NH_B, 2, 128), np.float32)
    for j in range(2):
        rel = (kp - qq) - 128 * j
        dbias[:, :, j, :] = t5[_t5_bucket(rel)].transpose(0, 2, 1)
    NEG = np.float32(-30000.0)
    dbias[64:128, :, 0, 0:64] = NEG
    dfar = np.broadcast_to(t5[15][None, :], (128, NH_B)).copy()
    brt = f(band_rel_table)
    bbias = np.zeros((DEPTH, NH_C, 128, 5, 128), np.float32)
    for j in range(5):
        idx = np.clip(128 * j + qq - kp, -256, 256) + 256
        bbias[:, :, :, j, :] = brt[:, :, idx]
    bbias[:, :, 64:128, 0, 0:64] = np.float32(-30000.0)
    bbias[:, :, 0:64, 4, 64:128] = np.float32(-30000.0)
    def tiles_col(W, starts, width):
        L_, K_, _ = W.shape
        nk_ = K_ // 128
        out = np.empty((L_, len(starts), 128, nk_ * width), np.float32)
        Wr = W.reshape(L_, nk_, 128, -1)
        for ti, st_ in enumerate(starts):
            out[:, ti] = Wr[:, :, :, st_:st_ + width].transpose(0, 2, 1, 3).reshape(L_, 128, nk_ * width)
        return out

    w_d_t = f(ffn_w_down).reshape(DEPTH, 22, 2, 128, D).transpose(0, 1, 3, 2, 4).reshape(DEPTH, 22, 128, 2 * D)
    shared = {
        "w_in": tiles_col(w_in, IN_STARTS, 256), "w_kr": tiles_col(f(w_kr), [0], 128), "w_uqn": tiles_col(f(w_uqn), [0], 768),
        "w_uqr": tiles_col(f(w_uqr), [0], 768), "w_ukk": tiles_col(f(w_ukk), [0], 768), "w_ukv": tiles_col(f(w_ukv), [0], 768),
        "w_o": tiles_col(f(w_o), [256 * i for i in range(8)], 256), "w_gu": tiles_col(w_gu, [256 * i for i in range(44)], 256),
        "w_d": np.ascontiguousarray(w_d_t),
        "pvec": pvec, "lamv": lamv, "cosT": f(cosT), "sinT": f(sinT), "dbias": dbias, "dfar": dfar, "bbias": bbias,
    }
    ckv, ckr = f(cache_mla_ckv), f(cache_mla_krope)
    cdk = f(cache_diff_k).reshape(DEPTH, 16, PAST, 512)
    cdv = f(cache_diff_v).reshape(DEPTH, 16, PAST, 512)
    cbk = f(cache_band_k).reshape(DEPTH, 16, 512, 768)
    cbv = f(cache_band_v).reshape(DEPTH, 16, 512, 768)
    ccv = f(state_ffn_conv)
    in_maps = []
    for i in range(8):
        m = dict(shared)
        m["x_p"] = x_prompt[i]
        m["x_s"] = np.ascontiguousarray(x_sample[2 * i:2 * i + 2].reshape(128, D))
        sl = slice(2 * i, 2 * i + 2)
        m["c_ckv"] = np.ascontiguousarray(ckv[:, sl])
        m["c_kr"] = np.ascontiguousarray(ckr[:, sl])
        m["c_dk"] = np.ascontiguousarray(cdk[:, sl])
        m["c_dv"] = np.ascontiguousarray(cdv[:, sl])
        m["c_bk"] = np.ascontiguousarray(cbk[:, sl])
        m["c_bv"] = np.ascontiguousarray(cbv[:, sl])
        m["c_conv"] = np.ascontiguousarray(ccv[:, sl])
        in_maps.append(m)
    return in_maps


def _assemble(R):
    st = lambda k: np.stack([np.asarray(r[k]) for r in R], 0)
    y_prompt = st("y_p")
    y_sample = st("y_s").reshape(16, DSEQ, D)
    pm = lambda k, shp: np.ascontiguousarray(np.moveaxis(st(k), 0, 1)).reshape(shp)
    p_ckv = pm("p_ckv", (DEPTH, 8, SEQ, 256))
    p_kr = pm("p_krope", (DEPTH, 8, SEQ, 64))
    p_dk = pm("p_dk", (DEPTH, 8, SEQ, NH_B, 128))
    p_dv = pm("p_dv", (DEPTH, 8, SEQ, NH_B, 128))
    p_bk = pm("p_bk", (DEPTH, 8, 512, NH_C, 128))
    p_bv = pm("p_bv", (DEPTH, 8, 512, NH_C, 128))
    p_cv = pm("p_conv", (DEPTH, 8, 2, DFF))
    s_ckv = pm("s_ckv", (DEPTH, 16, DSEQ, 256))
    s_kr = pm("s_krope", (DEPTH, 16, DSEQ, 64))
    s_dk = pm("s_dk", (DEPTH, 16, DSEQ, NH_B, 128))
    s_dv = pm("s_dv", (DEPTH, 16, DSEQ, NH_B, 128))
    s_bk = pm("s_bk", (DEPTH, 16, DSEQ, NH_C, 128))
    s_bv = pm("s_bv", (DEPTH, 16, DSEQ, NH_C, 128))
    s_cv = pm("s_conv", (DEPTH, 16, 2, DFF))
    return (y_prompt, y_sample, p_ckv, p_kr, p_dk, p_dv, p_bk, p_bv, p_cv, s_ckv, s_kr, s_dk, s_dv, s_bk, s_bv, s_cv)


def kernel(**inputs):
    in_maps = _host_prep(**inputs)
    if "nc" not in _CACHE:
        _, C0 = build(True, None, SUPERS, NLAYERS)
        nc, C1 = build(False, C0.wrec, SUPERS, NLAYERS)
        _CACHE["nc"] = nc
    nc = _CACHE["nc"]
    res = run_bass_kernel_spmd(nc, in_maps, core_ids=list(range(8)))
    return _assemble(res.results)
```

```python
import math
import numpy as np
import concourse.bass as bass
import concourse.mybir as mybir
from concourse.bass_utils import run_bass_kernel_spmd

F32 = mybir.dt.float32
BF = mybir.dt.bfloat16
AF = mybir.ActivationFunctionType
ALU = mybir.AluOpType

D = 2048
SEQ = 2048
DEPTH = 2
PAST = 2048
DSEQ = 64
NH_A, NH_B, NH_C = 6, 4, 6
DFF = 5632
NFF = DFF // 128
INC = 4672
ALPHA = (2 * DEPTH) ** 0.25
LN_EPS = 1e-5
RMS_EPS = 1e-6
O_CQ, O_CKV, O_KR, O_DQ, O_DK, O_DV, O_BQ, O_BK, O_BV = 0, 512, 768, 832, 1344, 1856, 2368, 3136, 3904
V_QG, V_KVG, V_SUB, V_L1G, V_L1B, V_L2G, V_L2B, V_CW, V_CB, V_END = 0, 4, 6, 7, 23, 39, 55, 71, 203, 247
IN_STARTS = [0, 256, 512, 832, 1088, 1344, 1600, 1856, 2112, 2368, 2624, 2880, 3136, 3392, 3648, 3904, 4160, 4416]
WTILES = {"w_in": (18, 16 * 256), "w_kr": (1, 16 * 128), "w_uqn": (1, 4 * 768), "w_uqr": (1, 4 * 768), "w_ukk": (1, 2 * 768),
          "w_ukv": (1, 2 * 768), "w_o": (8, 16 * 256), "w_gu": (44, 16 * 256), "w_d": (22, 2 * 2048)}


def _tile_of(key, kind, a, b, c):
    if kind == "row":
        return a // 2
    if key == "w_in":
        return IN_STARTS.index(b)
    if key in ("w_o", "w_gu"):
        return b // 256
    return 0


SUPERS = [0, 1, 2, 3, 4]
DBG = set()
NLAYERS = DEPTH


class Tok:
    __slots__ = ("sid", "sem", "val", "eng")

    def __init__(self, sid, sem, val, eng):
        self.sid, self.sem, self.val, self.eng = sid, sem, val, eng


class Buf:
    __slots__ = ("w", "r", "const", "excl")

    def __init__(self, const=False, excl=False):
        self.w = []
        self.r = []
        self.const = const
        self.excl = excl


class Eng:
    def __init__(self, C, name, h):
        self.C, self.name, self.h = C, name, h
        self.sem = None
        self.sid = None
        self.cnt = 0
        self.seen = {}
        self.pend = None

    def wait(self, tok):
        if tok is None or self.C.plan:
            return
        if tok.eng is self and self.name == "pe":
            return
        if tok.val is None:
            raise RuntimeError("wait on unresolved pending token (engine %s waits on %s)" % (self.name, tok.eng.name))
        if self.seen.get(tok.sid, 0) >= tok.val:
            return
        self.h.wait_ge(tok.sem, tok.val)
        self.seen[tok.sid] = tok.val

    def signal(self, ins):
        if self.sem is None or self.cnt >= 30000:
            self.sid, self.sem = self.C.newsem(self.name)
            self.cnt = 0
        self.cnt += 1
        ins.then_inc(self.sem, 1)
        tok = Tok(self.sid, self.sem, self.cnt, self)
        if self.pend is not None:
            self.pend.sid, self.pend.sem, self.pend.val = self.sid, self.sem, self.cnt
            self.pend = None
        return tok

    def pending(self):
        if self.pend is None:
            self.pend = Tok(None, None, None, self)
        return self.pend


class DmaQ:
    def __init__(self, C, eng, k):
        self.C, self.eng, self.k = C, eng, k
        self.sems = None
        self.i = 0


class Ctx:
    def __init__(self, nc, plan, wplan):
        self.nc = nc
        self.plan = plan
        self.wplan = wplan if wplan is not None else []
        self.wrec = []
        self.nsem = 0
        self.E = {n: Eng(self, n, h) for n, h in [("pe", nc.tensor), ("act", nc.scalar), ("dve", nc.vector),
                                                  ("pool", nc.gpsimd), ("sp", nc.sync)]}
        self.Q = {"sp": DmaQ(self, self.E["sp"], 16), "pool": DmaQ(self, self.E["pool"], 12)}
        self.dummy = Tok(0, None, 0, None)
        self.store_toks = []
        self.nops = 0

    def newsem(self, name):
        self.nsem += 1
        return self.nsem, self.nc.alloc_semaphore("s_%s_%d" % (name, self.nsem))

    def _deps(self, E, reads, writes):
        for b in reads:
            for t in b.w:
                E.wait(t)
            if b.excl:
                for t in b.r:
                    E.wait(t)
        for b in writes:
            for t in b.w:
                E.wait(t)
            for t in b.r:
                E.wait(t)

    def _post(self, tok, reads, writes, append):
        for b in reads:
            if not b.const and (not b.r or b.r[-1] is not tok):
                b.r.append(tok)
        for b in writes:
            if append:
                b.w.append(tok)
            else:
                b.w = [tok]
            b.r = []

    def op(self, en, emit, reads=(), writes=(), signal=True):
        self.nops += 1
        if self.plan:
            return self.dummy
        E = self.E[en]
        self._deps(E, reads, writes)
        ins = emit(E.h)
        tok = E.signal(ins) if signal else E.pending()
        self._post(tok, reads, writes, False)
        return tok

    def dma(self, qn, out, in_, reads=(), writes=(), append=False, store=False):
        self.nops += 1
        if self.plan:
            return self.dummy
        q = self.Q[qn]
        E = q.eng
        if q.sems is None:
            q.sems = [self.newsem("dq" + qn) for _ in range(q.k)]
        self._deps(E, reads, writes)
        k = q.i % q.k
        n = q.i // q.k + 1
        q.i += 1
        sid, sem = q.sems[k]
        if n > 1:
            E.wait(Tok(sid, sem, 16 * (n - 1), None))
        ins = E.h.dma_start(out=out, in_=in_)
        ins.then_inc(sem, 16)
        tok = Tok(sid, sem, 16 * n, None)
        self._post(tok, reads, writes, append)
        if store:
            self.store_toks.append(tok)
        return tok


class Ring:
    def __init__(self, items):
        self.items = items
        self.i = 0

    def next(self):
        it = self.items[self.i % len(self.items)]
        self.i += 1
        return it


def build(plan, wplan, supers, nlayers, stop=99):
    nc = bass.Bass("TRN2", target_bir_lowering=False)
    C = Ctx(nc, plan, wplan)
    op, dma = C.op, C.dma

    def din(name, shape, dt=F32):
        return nc.dram_tensor(name, list(shape), dt, kind="ExternalInput").ap()

    def dout(name, shape):
        return nc.dram_tensor(name, list(shape), F32, kind="ExternalOutput").ap()

    def dscr(name, shape, dt=BF):
        return nc.dram_tensor(name, list(shape), dt).ap()

    x_p = din("x_p", [SEQ, D])
    x_s = din("x_s", [128, D])
    c_ckv = din("c_ckv", [DEPTH, 2, PAST, 256])
    c_kr = din("c_kr", [DEPTH, 2, PAST, 64])
    c_dk = din("c_dk", [DEPTH, 2, PAST, 512])
    c_dv = din("c_dv", [DEPTH, 2, PAST, 512])
    c_bk = din("c_bk", [DEPTH, 2, 512, 768])
    c_bv = din("c_bv", [DEPTH, 2, 512, 768])
    c_conv = din("c_conv", [DEPTH, 2, 2, DFF])
    W32 = {k_: din(k_, [DEPTH, nt_, 128, x_]) for k_, (nt_, x_) in WTILES.items()}
    pvec_d = din("pvec", [128, DEPTH, V_END])
    lamv_d = din("lamv", [128, DEPTH, 4, 64])
    cos_d = din("cosT", [64, SEQ + DSEQ])
    sin_d = din("sinT", [64, SEQ + DSEQ])
    dbias_d = din("dbias", [128, NH_B, 2, 128])
    dfar_d = din("dfar", [128, NH_B])
    bbias_d = din("bbias", [DEPTH, NH_C, 128, 5, 128])

    y_p = dout("y_p", [SEQ, D])
    y_s = dout("y_s", [128, D])
    OUTP = {"ckv": dout("p_ckv", [DEPTH, SEQ, 256]), "kr": dout("p_krope", [DEPTH, SEQ, 64]),
            "dk": dout("p_dk", [DEPTH, SEQ, 512]), "dv": dout("p_dv", [DEPTH, SEQ, 512]),
            "bk": dout("p_bk", [DEPTH, 512, 768]), "bv": dout("p_bv", [DEPTH, 512, 768])}
    p_conv = dout("p_conv", [DEPTH, 2, DFF])
    OUTS = {"ckv": dout("s_ckv", [DEPTH, 128, 256]), "kr": dout("s_krope", [DEPTH, 128, 64]),
            "dk": dout("s_dk", [DEPTH, 128, 512]), "dv": dout("s_dv", [DEPTH, 128, 512]),
            "bk": dout("s_bk", [DEPTH, 128, 768]), "bv": dout("s_bv", [DEPTH, 128, 768])}
    s_conv = dout("s_conv", [DEPTH, 2, 2, DFF])

    WB = {k: dscr("b_" + k, v.shape) for k, v in W32.items()}
    WBb = {k: Buf() for k in W32}
    LK = [SEQ, PAST + DSEQ, PAST + DSEQ]
    LKB = [SEQ, 512 + DSEQ, 512 + DSEQ]
    SC = {}
    SCB = {}
    for l in range(DEPTH):
        for q in range(3):
            SC["akT", l, q] = dscr("akT_%d_%d" % (l, q), [NH_A, 128, LK[q]])
            SC["kr", l, q] = dscr("kr_%d_%d" % (l, q), [64, LK[q]])
            SC["av", l, q] = dscr("av_%d_%d" % (l, q), [LK[q], 768])
            SC["dkT", l, q] = dscr("dkT_%d_%d" % (l, q), [NH_B, 128, LK[q]])
            SC["dv", l, q] = dscr("dv_%d_%d" % (l, q), [LK[q], 512])
            SC["bkT", l, q] = dscr("bkT_%d_%d" % (l, q), [NH_C, 128, LKB[q]])
            SC["bv", l, q] = dscr("bv_%d_%d" % (l, q), [LKB[q], 768])
            for k in ("akT", "kr", "av", "dkT", "dv", "bkT", "bv"):
                SCB[k, l, q] = Buf()

    def sb(name, shape, dt):
        return nc.alloc_sbuf_tensor("sb_" + name, list(shape), dt)

    TMAX = 512
    xres = sb("xres", [128, 16, TMAX], F32)
    xres_b = [Buf() for _ in range(16)]
    xT = sb("xT", [128, 16, TMAX], BF)
    xT_b = [Buf() for _ in range(16)]
    oT = sb("oT", [128, 16, TMAX], BF)
    oT_b = [Buf() for _ in range(16)]
    qa = sb("qa", [128, 6, TMAX], BF)
    qa_b = [Buf() for _ in range(6)]
    qr = sb("qr", [64, 6, TMAX], BF)
    qr_b = [Buf() for _ in range(6)]
    cqT = sb("cqT", [128, 4, TMAX], BF)
    cqT_b = [Buf() for _ in range(4)]
    ckvT = sb("ckvT", [128, 2, TMAX], BF)
    ckvT_b = [Buf() for _ in range(2)]
    ckvf = sb("ckvf", [128, 2, TMAX], F32)
    ckvf_b = [Buf() for _ in range(2)]
    krf = sb("krf", [64, TMAX], F32)
    krf_b = Buf()
    NWS = 3
    wslots = Ring([(sb("wsl%d" % i, [128, 4096], BF), Buf()) for i in range(NWS)])
    hbr = Ring([(sb("hb%d" % i, [128, TMAX], F32), Buf()) for i in range(2)])
    stgr = Ring([(sb("stg%d" % i, [128, 1024], F32), Buf()) for i in range(2)])
    vst = sb("vst", [128, 4, 768], BF)
    vst_b = Buf()
    kbr = Ring([(sb("kb%d" % i, [128, PAST + DSEQ], BF), Buf()) for i in range(2)])
    krb = sb("krb", [64, PAST + DSEQ], BF)
    krb_b = Buf()
    vbr = Ring([(sb("vb%d" % i, [128, 17, 128], BF), Buf()) for i in range(2)])
    ptr = Ring([(sb("pt%d" % i, [128, TMAX], BF), Buf()) for i in range(6)])
    tsr = Ring([(sb("ts%d" % i, [128, 128], F32), Buf()) for i in range(3)])
    f32r = Ring([(sb("ft%d" % i, [128, TMAX], F32), Buf()) for i in range(6)])
    cosb = sb("cosb", [64, TMAX], F32)
    sinb = sb("sinb", [64, TMAX], F32)
    cs_b = Buf()
    nm0 = sb("nm0", [128, TMAX], F32)
    nm1 = sb("nm1", [128, TMAX], F32)
    nm0_b, nm1_b = Buf(), Buf()
    gpr = Ring([(sb("gp%d" % i, [128, TMAX + 8], F32), Buf()) for i in range(2)])
    hTr = Ring([(sb("hT%d" % i, [128, 2, TMAX], BF), Buf()) for i in range(2)])
    dbias = sb("dbias", [128, NH_B, 2, 128], F32)
    dfar = sb("dfar", [128, NH_B], F32)
    bbr = Ring([(sb("bb%d" % i, [128, 5, 128], F32), Buf()) for i in range(2)])
    pvec = sb("pvec", [128, DEPTH, V_END], F32)
    lamv = sb("lamv", [128, DEPTH, 4, 64], F32)
    lamt = sb("lamt", [128, 64], F32)
    lams = sb("lams", [128, 16], F32)
    agb = sb("agb", [128, DEPTH, 32], F32)
    ident = sb("ident", [128, 128], F32)
    onesD = sb("onesD", [128, 128], F32)
    onesB = sb("onesB", [128, 128], BF)
    ones512 = sb("ones512", [128, 128], F32)
    ones256 = sb("ones256", [128, 128], F32)
    ones128 = sb("ones128", [128, 128], F32)
    carry = sb("carry", [128, DEPTH, 2, 2, NFF], F32)
    carry_b = [Buf() for _ in range(DEPTH)]
    cvst = sb("cvst", [NFF, 2, 2, 128], F32)
    cvst_b = Buf()
    const_b = Buf(const=True)

    banks = [(nc.alloc_psum_tensor("ps%d" % i, [128, 512], F32), Buf(excl=True)) for i in range(8)]
    ps_short = Ring(banks[0:4])
    ps_long = Ring(banks[4:8])

    if "noconst" not in DBG:
        dma("sp", pvec[:], pvec_d[:, :, :], writes=[const_b], append=True)
        dma("sp", lamv[:], lamv_d[:, :, :, :], writes=[const_b], append=True)
        dma("sp", dbias[:], dbias_d[:, :, :, :], writes=[const_b], append=True)
        dma("sp", dfar[:], dfar_d[:, :], writes=[const_b], append=True)
    cb0 = Buf()
    op("pool", lambda h: h.memset(ident[:], 0.0), writes=[cb0])
    op("pool", lambda h: h.affine_select(out=ident[:], in_=ident[:], pattern=[[-1, 128]], compare_op=ALU.not_equal,
                                         fill=1.0, base=0, channel_multiplier=1), reads=[cb0], writes=[cb0])
    for t_, v_ in ((onesD, 1.0 / D), (onesB, 1.0), (ones512, 1.0 / 512), (ones256, 1.0 / 256), (ones128, 1.0 / 128)):
        op("pool", lambda h, t_=t_, v_=v_: h.memset(t_[:], v_), writes=[cb0])
    op("pool", lambda h: h.memset(carry[:], 0.0), writes=carry_b)
    const_b.w.extend(cb0.w)
    for l in ([] if "nolam" in DBG else range(DEPTH)):
        lam_init = 0.8 - 0.6 * math.exp(-0.3 * l)
        tb = Buf()
        for i in range(2):
            op("dve", lambda h, i=i, l=l: h.tensor_tensor(out=lamt[:, :], in0=lamv[:, l, 2 * i, :], in1=lamv[:, l, 2 * i + 1, :],
                                                           op=ALU.mult), reads=[const_b], writes=[tb])
            op("dve", lambda h, i=i, l=l: h.reduce_sum(out=lams[:, l * 8 + i:l * 8 + i + 1], in_=lamt[:, :],
                                                        axis=mybir.AxisListType.X), reads=[tb], writes=[tb])
            op("act", lambda h, i=i, l=l: h.activation(out=lams[:, l * 8 + 2 + i:l * 8 + 3 + i],
                                                        in_=lams[:, l * 8 + i:l * 8 + i + 1], func=AF.Exp),
               reads=[tb], writes=[tb])
        op("dve", lambda h, l=l, li=lam_init: h.scalar_tensor_tensor(
            out=lams[:, l * 8 + 4:l * 8 + 5], in0=lams[:, l * 8 + 3:l * 8 + 4], scalar=-li,
            in1=lams[:, l * 8 + 2:l * 8 + 3], op0=ALU.add, op1=ALU.subtract), reads=[tb], writes=[tb])
        op("dve", lambda h, l=l, li=lam_init: h.tensor_scalar(
            out=lams[:, l * 8 + 5:l * 8 + 6], in0=pvec[:, l, V_SUB:V_SUB + 1], scalar1=(1.0 - li), scalar2=None,
            op0=ALU.mult), reads=[const_b, tb], writes=[tb])
        op("dve", lambda h, l=l: h.tensor_scalar(out=agb[:, l, :], in0=pvec[:, l, V_L1G:V_L1G + 32], scalar1=ALPHA,
                                                 scalar2=None, op0=ALU.mult), reads=[const_b, tb], writes=[tb])
        const_b.w.extend(tb.w)

    def ps_view(bank, m, n):
        return bank[0:m, 0:n]

    wstate = {"issued": 0, "used": 0, "slots": {}, "cast_next": 0}
    castbuf = {}
    CAST_LA = 4

    def _cast_upto(idx):
        while wstate["cast_next"] <= min(idx, len(C.wplan) - 1):
            ent = C.wplan[wstate["cast_next"]]
            wstate["cast_next"] += 1
            if ent in castbuf:
                continue
            key, l, kind, a, b, c = ent
            cb = Buf()
            castbuf[ent] = cb
            ti = _tile_of(key, kind, a, b, c)
            dma("pool", WB[key][l, ti, :, :], W32[key][l, ti, :, :], writes=[cb])

    def _issue_w(idx):
        key, l, kind, a, b, c = C.wplan[idx]
        _cast_upto(idx + CAST_LA)
        WBb[key] = castbuf[C.wplan[idx]]
        t, bf = wslots.next()
        nk_, nc_ = (a, c) if kind == "col" else (b, c)
        assert nk_ * nc_ == WTILES[key][1], (key, nk_, nc_)
        view = t[:, 0:nk_ * nc_].rearrange("p (k c) -> p k c", k=nk_)
        dma("sp", t[:, 0:nk_ * nc_], WB[key][l, _tile_of(key, kind, a, b, c), :, :], reads=[WBb[key]], writes=[bf])
        wstate["slots"][idx] = (view, bf)

    def load_w(key, l, kind, a, b, c):
        idx = wstate["used"]
        wstate["used"] += 1
        if C.plan:
            C.wrec.append((key, l, kind, a, b, c))
            t, bf = wslots.items[0]
            k = a if kind == "col" else b
            return t[:, 0:k * c].rearrange("p (k c) -> p k c", k=k), bf
        assert C.wplan[idx] == (key, l, kind, a, b, c), (idx, C.wplan[idx], (key, l, kind, a, b, c))
        while wstate["issued"] < min(len(C.wplan), idx + NWS):
            _issue_w(wstate["issued"])
            wstate["issued"] += 1
        return wstate["slots"].pop(idx)

    def mm_acc(bank_b, out_ap, pairs, reads):
        n = len(pairs)
        tok = None
        for i, pr in enumerate(pairs):
            lt, rh = pr[0], pr[1]
            rd = list(reads) + (list(pr[2]) if len(pr) > 2 else [])
            tok = op("pe", lambda h, lt=lt, rh=rh, i=i: h.matmul(out_ap, lhsT=lt, rhs=rh, start=(i == 0), stop=(i == n - 1)),
                     reads=rd, writes=[bank_b], signal=(i == n - 1))
        return tok

    def transpose_to(bank, bank_b, out_ap, in_ap, npart, in_bufs, first=True, signal=True):
        return op("pe", lambda h: h.transpose(out=out_ap, in_=in_ap, identity=ident[0:npart, 0:npart]),
                  reads=[const_b] + in_bufs, writes=[bank_b], signal=signal)

    def rstd_from(ps_ap, psb, out_t, out_b, n, eps, T):
        op("act", lambda h: h.activation(out=out_t[0:n, 0:T], in_=ps_ap, func=AF.Sqrt, bias=eps, scale=1.0),
           reads=[psb], writes=[out_b])
        op("dve", lambda h: h.reciprocal(out=out_t[0:n, 0:T], in_=out_t[0:n, 0:T]), reads=[out_b], writes=[out_b])

    class RowStage:
        def __init__(self, NT, dests, width):
            self.NT, self.dests, self.width = NT, dests, width
            self.cur = None
            self.col0 = 0
            self.fill = 0

        def add(self, src_ap, npart, src_bufs):
            NT = self.NT
            if self.cur is None:
                self.cur = stgr.next()
                self.fill = 0
            st, stb = self.cur
            sv = st[:, 0:NT * 256].rearrange("p (t c) -> p t c", t=NT)
            bank, bb = ps_long.next()
            pv = bank[:, 0:NT * 128].rearrange("p (t c) -> p t c", t=NT)
            for tt in range(NT):
                transpose_to(bank, bb, pv[:, tt, 0:npart], src_ap[:, tt * 128:(tt + 1) * 128], npart, src_bufs,
                             signal=(tt == NT - 1))
            f = self.fill
            op("act", lambda h: h.copy(out=sv[:, :, f:f + npart], in_=pv[:, :, 0:npart]), reads=[bb], writes=[stb])
            self.fill += npart
            if self.fill >= 256 or self.col0 + self.fill >= self.width:
                self.flush()

        def flush(self):
            if self.cur is None or self.fill == 0:
                return
            NT = self.NT
            st, stb = self.cur
            sv = st[:, 0:NT * 256].rearrange("p (t c) -> p t c", t=NT)
            c0, f = self.col0, self.fill
            for dst, dbuf, is_out in self.dests:
                dv = dst.rearrange("(t p) c -> p t c", p=128)[:, :, c0:c0 + f]
                dma("pool", dv, sv[:, :, 0:f], reads=[stb], writes=([dbuf] if dbuf is not None else []), append=True,
                    store=is_out)
            self.col0 += f
            self.cur = None
            self.fill = 0

    def attn_scores_exp(h_ap_pairs, reads, nk, N, bias_fn, scale, mask_fn, bias_bufs=()):
        bank, bb = ps_short.next()
        mm_acc(bb, bank[0:nk, 0:N], h_ap_pairs, reads)
        pt, ptb = ptr.next()
        segs = bias_fn() if bias_fn is not None else [(0, N, None, None)]
        for (c0, ncs, bias_ap, far_ap) in segs:
            if bias_ap is not None:
                ts, tsb = tsr.next()
                op("dve", lambda h, c0=c0, ncs=ncs, bias_ap=bias_ap, ts=ts: h.scalar_tensor_tensor(
                    out=ts[0:nk, 0:ncs], in0=bank[0:nk, c0:c0 + ncs], scalar=scale, in1=bias_ap, op0=ALU.mult,
                    op1=ALU.add), reads=[bb, const_b] + list(bias_bufs), writes=[tsb])
                op("act", lambda h, c0=c0, ncs=ncs, ts=ts: h.activation(out=pt[0:nk, c0:c0 + ncs], in_=ts[0:nk, 0:ncs],
                                                                       func=AF.Exp), reads=[tsb], writes=[ptb])
            elif far_ap is not None:
                op("act", lambda h, c0=c0, ncs=ncs, far_ap=far_ap: h.activation(
                    out=pt[0:nk, c0:c0 + ncs], in_=bank[0:nk, c0:c0 + ncs], func=AF.Exp, bias=far_ap, scale=scale),
                   reads=[bb, const_b], writes=[ptb])
            else:
                op("act", lambda h, c0=c0, ncs=ncs: h.activation(out=pt[0:nk, c0:c0 + ncs], in_=bank[0:nk, c0:c0 + ncs],
                                                                func=AF.Exp, scale=scale), reads=[bb], writes=[ptb])
        for (p0, p1, c0, c1) in (mask_fn() if mask_fn is not None else []):
            op("pool", lambda h, p0=p0, p1=p1, c0=c0, c1=c1: h.memset(pt[p0:p1, c0:c1], 0.0), writes=[ptb])
        return pt, ptb

    class AU:
        __slots__ = ("pairs", "reads", "nk", "N", "bias_fn", "scale", "mask_fn", "bias_bufs", "acc", "oc0", "vt", "first", "last")

        def __init__(self, **kw):
            self.bias_fn = None
            self.mask_fn = None
            self.bias_bufs = ()
            for k_, v_ in kw.items():
                setattr(self, k_, v_)

    def attn_run(nheads, load_fn, units_fn, nacc, fin_fn, depth=3, late_lag=6):
        pipe = []
        late = []

        def tick_late(force=False):
            for it in list(late):
                it[0] -= 1
                if force or it[0] <= 0:
                    late.remove(it)
                    it[1]()

        cur = load_fn(0)
        for h_ in range(nheads):
            units = units_fn(h_, cur)
            accs = [ps_long.next() for _ in range(nacc)]
            nxt = None
            if len(units) <= depth:
                while pipe:
                    pipe.pop(0)()
            for ui, u in enumerate(units):
                pt, ptb = attn_scores_exp(u.pairs, u.reads, u.nk, u.N, u.bias_fn, u.scale, u.mask_fn, u.bias_bufs)

                def stage2(u=u, pt=pt, ptb=ptb, accs=accs, h_=h_, lastu=(ui == len(units) - 1), cur=cur):
                    vb, vbb = cur[2], cur[3]
                    ob, obb = accs[u.acc]
                    sb_, sbb = accs[u.acc + nacc // 2]
                    op("pe", lambda h: h.matmul(ob[:, u.oc0:u.oc0 + u.N], lhsT=vb[0:u.nk, u.vt, :], rhs=pt[0:u.nk, 0:u.N],
                                                start=u.first, stop=u.last), reads=[vbb, ptb], writes=[obb], signal=False)
                    op("pe", lambda h: h.matmul(sb_[:, u.oc0:u.oc0 + u.N], lhsT=onesB[0:u.nk, :], rhs=pt[0:u.nk, 0:u.N],
                                                start=u.first, stop=u.last), reads=[const_b, ptb], writes=[sbb], signal=True)
                    if lastu:
                        tick_late(force=True)
                        cont = fin_fn(h_, accs)
                        if cont is not None:
                            late.append([late_lag, cont])

                pipe.append(stage2)
                if len(pipe) > depth:
                    pipe.pop(0)()
                tick_late()
                if ui == min(depth, len(units) - 1) and h_ + 1 < nheads:
                    nxt = load_fn(h_ + 1)
            cur = nxt
        while pipe:
            pipe.pop(0)()
        tick_late(force=True)

    def softmax_fin(oacc, sacc, out_ap, out_b, ncs):
        (ob, obb), (sb_, sbb) = oacc, sacc
        fo, fob = f32r.next()
        fs, fsb = f32r.next()
        op("act", lambda h: h.copy(out=fo[:, 0:ncs], in_=ob[:, 0:ncs]), reads=[obb], writes=[fob])
        op("dve", lambda h: h.tensor_copy(out=fs[:, 0:ncs], in_=sb_[:, 0:ncs]), reads=[sbb], writes=[fsb])
        op("dve", lambda h: h.reciprocal(out=fs[:, 0:ncs], in_=fs[:, 0:ncs]), reads=[fsb], writes=[fsb])
        op("pool", lambda h: h.tensor_tensor(out=out_ap, in0=fo[:, 0:ncs], in1=fs[:, 0:ncs], op=ALU.mult), reads=[fob, fsb],
           writes=[out_b])

    def load_kv(kind_k, kind_v, l, q, h, k_lo, k_hi, vcol0):
        kb, kbb = kbr.next()
        vb, vbb = vbr.next()
        n = k_hi - k_lo
        dma("sp", kb[:, 0:n], SC[kind_k, l, q][h, :, k_lo:k_hi], reads=[SCB[kind_k, l, q]], writes=[kbb])
        nfull = n // 128
        src = SC[kind_v, l, q]
        if nfull:
            dma("sp", vb[:, 0:nfull, :], src[k_lo:k_lo + nfull * 128, vcol0:vcol0 + 128].rearrange("(t p) c -> p t c", p=128),
                reads=[SCB[kind_v, l, q]], writes=[vbb])
        rem = n - nfull * 128
        if rem:
            dma("sp", vb[0:rem, nfull, :], src[k_lo + nfull * 128:k_hi, vcol0:vcol0 + 128], reads=[SCB[kind_v, l, q]],
                writes=[vbb], append=True)
        return kb, kbb, vb, vbb

    def layer(s, l, T, segs, pos0):
        NT = T // 128
        is_p = (s < 4)
        t0 = s * 512 if is_p else 0
        OUT = OUTP if is_p else OUTS
        pv = lambda a, n=1: pvec[:, l, a:a + n]
        kq = lambda kind, kp: (512 if ((not is_p) and kind in ("bkT", "bv")) else kp)

        def out_rows(name):
            if is_p:
                return OUT[name][l, t0:t0 + T, :]
            return OUT[name][l, :, :]

        def scr_rows(kind, width):
            res = []
            for (q, c0, ncs, kp) in segs:
                res.append((SC[kind, l, q][kp:kp + ncs, :], SCB[kind, l, q]))
            return res

        def gemm_fm(key, col0, ncols, consumer, nk=16, rhs=None, rhs_b=None, wcol=256, lag=2):
            rhs = xT if rhs is None else rhs
            rhs_b = xT_b if rhs_b is None else rhs_b
            c = col0
            ci = 0
            pend = []
            while c < col0 + ncols:
                wc = min(wcol, col0 + ncols - c)
                wv, wb = load_w(key, l, "col", nk, c, wc)
                for j in range(0, wc, 128):
                    m = min(128, wc - j)
                    bank, bb = ps_short.next()
                    mm_acc(bb, bank[0:m, 0:T], [(wv[:, k, j:j + m], rhs[:, k, 0:T], [rhs_b[k]]) for k in range(nk)], [wb])
                    pend.append((ci, bank, bb, m))
                    if len(pend) > lag:
                        consumer(*pend.pop(0))
                    ci += 1
                c += wc
            while pend:
                consumer(*pend.pop(0))

        hT0, hT0b = hTr.items[0]
        hT1, hT1b = hTr.items[1]
        dq_t = lambda hh: (hT0 if hh < 2 else hT1)
        dq_b = lambda hh: (hT0b if hh < 2 else hT1b)

        def cons_dq(ci, bank, bb, m):
            op("act", lambda h: h.copy(out=dq_t(ci)[:, ci % 2, 0:T], in_=bank[:, 0:T]), reads=[bb], writes=[dq_b(ci)])

        def cons_q(ci, bank, bb, m):
            op("act", lambda h: h.copy(out=qa[:, ci, 0:T], in_=bank[:, 0:T]), reads=[bb], writes=[qa_b[ci]])

        def make_cons_k(kind, rs):
            def cons(ci, bank, bb, m):
                hb, hbb = hbr.next()
                op("act" if ci % 2 else "dve", lambda h: (h.copy(out=hb[:, 0:T], in_=bank[:, 0:T]) if ci % 2 else
                                                           h.tensor_copy(out=hb[:, 0:T], in_=bank[:, 0:T])), reads=[bb], writes=[hbb])
                for (q, c0, ncs, kp) in segs:
                    kp = kq(kind, kp)
                    dma("pool", SC[kind, l, q][ci, :, kp:kp + ncs], hb[:, c0:c0 + ncs], reads=[hbb], writes=[SCB[kind, l, q]],
                        append=True)
                if rs is not None:
                    rs.add(hb[:, 0:T], 128, [hbb])
            return cons

        def make_cons_v(rs):
            def cons(ci, bank, bb, m):
                hb, hbb = hbr.next()
                op("dve", lambda h: h.tensor_copy(out=hb[:, 0:T], in_=bank[:, 0:T]), reads=[bb], writes=[hbb])
                rs.add(hb[:, 0:T], 128, [hbb])
            return cons

        def vdests(kind, outname, width, want_out):
            d = []
            if want_out:
                d.append((out_rows(outname), None, True))
            return d

        class RowStageV(RowStage):
            def __init__(self, NT, dests, width, kind):
                RowStage.__init__(self, NT, dests, width)
                self.kind = kind

            def flush(self):
                if self.cur is None or self.fill == 0:
                    return
                st, stb = self.cur
                sv = st[:, 0:self.NT * 256].rearrange("p (t c) -> p t c", t=self.NT)
                cc, f = self.col0, self.fill
                for (q, c0, ncs, kp) in segs:
                    kp = kq(self.kind, kp)
                    if ncs >= 128:
                        dma("pool", SC[self.kind, l, q][kp:kp + ncs, cc:cc + f].rearrange("(t p) c -> p t c", p=128),
                            sv[:, :, 0:f], reads=[stb], writes=[SCB[self.kind, l, q]], append=True)
                    else:
                        dma("pool", SC[self.kind, l, q][kp:kp + ncs, cc:cc + f], sv[c0:c0 + ncs, 0, 0:f], reads=[stb],
                            writes=[SCB[self.kind, l, q]], append=True)
                RowStage.flush(self)

        for ii_, (q, c0, ncs, kp) in enumerate(segs):
            dma("sp", cosb[:, c0:c0 + ncs], cos_d[:, kp:kp + ncs], writes=[cs_b], append=(ii_ > 0))
            dma("sp", sinb[:, c0:c0 + ncs], sin_d[:, kp:kp + ncs], writes=[cs_b], append=True)

        ssq_bank, ssq_b = ps_long.next()
        sq_list = []

        def cons_cq(ci, bank, bb, m):
            ft, ftb = f32r.next()
            op("act", lambda h: h.activation(out=ft[:, 0:T], in_=bank[:, 0:T], func=AF.Square), reads=[bb], writes=[ftb])
            op("dve", lambda h: h.tensor_scalar(out=cqT[:, ci, 0:T], in0=bank[:, 0:T], scalar1=pv(V_QG + ci), scalar2=None,
                                                op0=ALU.mult), reads=[bb, const_b], writes=[cqT_b[ci]])
            op("pe", lambda h: h.matmul(ssq_bank[:, 0:T], lhsT=ones512[:, :], rhs=ft[:, 0:T], start=(ci == 0), stop=(ci == 3)),
               reads=[ftb, const_b], writes=[ssq_b], signal=True)

        gemm_fm("w_in", O_CQ, 512, cons_cq)
        rstd_from(ssq_bank[:, 0:T], ssq_b, nm0, nm0_b, 128, RMS_EPS, T)

        ssk_bank, ssk_b = ps_long.next()

        def cons_ckv(ci, bank, bb, m):
            ft, ftb = f32r.next()
            op("act", lambda h: h.activation(out=ft[:, 0:T], in_=bank[:, 0:T], func=AF.Square), reads=[bb], writes=[ftb])
            op("dve", lambda h: h.tensor_copy(out=ckvf[:, ci, 0:T], in_=bank[:, 0:T]), reads=[bb], writes=[ckvf_b[ci]])
            op("pe", lambda h: h.matmul(ssk_bank[:, 0:T], lhsT=ones256[:, :], rhs=ft[:, 0:T], start=(ci == 0), stop=(ci == 1)),
               reads=[ftb, const_b], writes=[ssk_b], signal=True)

        gemm_fm("w_in", O_CKV, 256, cons_ckv)
        rstd_from(ssk_bank[:, 0:T], ssk_b, nm1, nm1_b, 128, RMS_EPS, T)
        gemm_fm("w_in", O_DQ, 512, cons_dq)
        rs = RowStage(NT, [(out_rows("dk"), None, True)], 512)
        gemm_fm("w_in", O_DK, 512, make_cons_k("dkT", rs))
        rs.flush()
        rs = RowStageV(NT, [(out_rows("dv"), None, True)], 512, "dv")
        gemm_fm("w_in", O_DV, 512, make_cons_v(rs))
        rs.flush()
        rs_ckv = RowStage(NT, [(out_rows("ckv"), None, True)], 256)
        for ci in range(2):
            op("dve", lambda h, ci=ci: h.tensor_tensor(out=ckvf[:, ci, 0:T], in0=ckvf[:, ci, 0:T], in1=nm1[:, 0:T], op=ALU.mult),
               reads=[nm1_b], writes=[ckvf_b[ci]])
            op("dve", lambda h, ci=ci: h.tensor_scalar(out=ckvf[:, ci, 0:T], in0=ckvf[:, ci, 0:T], scalar1=pv(V_KVG + ci),
                                                       scalar2=None, op0=ALU.mult), reads=[const_b], writes=[ckvf_b[ci]])
            op("act", lambda h, ci=ci: h.copy(out=ckvT[:, ci, 0:T], in_=ckvf[:, ci, 0:T]), reads=[ckvf_b[ci]], writes=[ckvT_b[ci]])
            rs_ckv.add(ckvf[:, ci, 0:T], 128, [ckvf_b[ci]])
        rs_ckv.flush()

        if stop <= 3:
            return
        def cons_kn(ci, bank, bb, m):
            pt, ptb = ptr.next()
            op("act", lambda h: h.copy(out=pt[:, 0:T], in_=bank[:, 0:T]), reads=[bb], writes=[ptb])
            for (q, c0, ncs, kp) in segs:
                dma("pool", SC["akT", l, q][ci, :, kp:kp + ncs], pt[:, c0:c0 + ncs], reads=[ptb], writes=[SCB["akT", l, q]],
                    append=True)

        gemm_fm("w_ukk", 0, 768, cons_kn, nk=2, rhs=ckvT, rhs_b=ckvT_b, wcol=768)
        wv, wb = load_w("w_ukv", l, "col", 2, 0, 768)
        for tt in range(NT):
            for (cc0, cw) in ((0, 512), (512, 256)):
                bank, bb = ps_short.next()
                mm_acc(bb, bank[:, 0:cw], [(ckvT[:, k, tt * 128:(tt + 1) * 128], wv[:, k, cc0:cc0 + cw]) for k in range(2)],
                       [wb] + ckvT_b)
                op("act" if cc0 else "dve", lambda h, bank=bank, cc0=cc0, cw=cw, tt=tt: (
                    h.copy(out=vst[:, tt, cc0:cc0 + cw], in_=bank[:, 0:cw]) if cc0 else
                    h.tensor_copy(out=vst[:, tt, cc0:cc0 + cw], in_=bank[:, 0:cw])), reads=[bb], writes=[vst_b])
        for (q, c0, ncs, kp) in segs:
            if ncs >= 128:
                dma("pool", SC["av", l, q][kp:kp + ncs, :].rearrange("(t p) c -> p t c", p=128), vst[:, 0:NT, :], reads=[vst_b],
                    writes=[SCB["av", l, q]], append=True)
            else:
                dma("pool", SC["av", l, q][kp:kp + ncs, :], vst[c0:c0 + ncs, 0, :], reads=[vst_b], writes=[SCB["av", l, q]],
                    append=True)

        if stop <= 1:
            return
        def rope_from(bankA, bbA, bankB, bbB, out_ap, out_b, extra_scale=None):
            f1, f1b = f32r.next()
            f2, f2b = f32r.next()
            op("dve", lambda h: h.tensor_tensor(out=f1[0:64, 0:T], in0=bankA[0:64, 0:T], in1=cosb[:, 0:T], op=ALU.mult),
               reads=[bbA, cs_b], writes=[f1b])
            op("dve", lambda h: h.tensor_tensor(out=f2[0:64, 0:T], in0=bankB[0:64, 0:T], in1=sinb[:, 0:T], op=ALU.mult),
               reads=[bbB, cs_b], writes=[f2b])
            if extra_scale is None:
                op("dve", lambda h: h.tensor_tensor(out=out_ap, in0=f1[0:64, 0:T], in1=f2[0:64, 0:T], op=ALU.add),
                   reads=[f1b, f2b], writes=[out_b])
            else:
                op("dve", lambda h: h.tensor_tensor(out=f1[0:64, 0:T], in0=f1[0:64, 0:T], in1=f2[0:64, 0:T], op=ALU.add),
                   reads=[f2b], writes=[f1b])
                op("dve", lambda h: h.tensor_tensor(out=out_ap, in0=f1[0:64, 0:T], in1=extra_scale, op=ALU.mult),
                   reads=[f1b, nm0_b], writes=[out_b])

        wv, wb = load_w("w_kr", l, "col", 16, 0, 128)
        bA, bbA = ps_short.next()
        mm_acc(bbA, bA[0:64, 0:T], [(wv[:, k, 0:64], xT[:, k, 0:T]) for k in range(16)], [wb] + xT_b)
        bB, bbB = ps_short.next()
        mm_acc(bbB, bB[0:64, 0:T], [(wv[:, k, 64:128], xT[:, k, 0:T]) for k in range(16)], [wb] + xT_b)
        rope_from(bA, bbA, bB, bbB, krf[:, 0:T], krf_b)
        rs_kr = RowStage(NT, [(out_rows("kr"), None, True)], 64)
        rs_kr.add(krf[:, 0:T], 64, [krf_b])
        rs_kr.flush()
        pt, ptb = ptr.next()
        op("act", lambda h: h.copy(out=pt[0:64, 0:T], in_=krf[:, 0:T]), reads=[krf_b], writes=[ptb])
        for (q, c0, ncs, kp) in segs:
            dma("pool", SC["kr", l, q][:, kp:kp + ncs], pt[0:64, c0:c0 + ncs], reads=[ptb], writes=[SCB["kr", l, q]], append=True)

        if stop <= 2:
            return
        def cons_qn(ci, bank, bb, m):
            op("dve", lambda h: h.tensor_tensor(out=qa[:, ci, 0:T], in0=bank[:, 0:T], in1=nm0[:, 0:T], op=ALU.mult),
               reads=[bb, nm0_b], writes=[qa_b[ci]])

        gemm_fm("w_uqn", 0, 768, cons_qn, nk=4, rhs=cqT, rhs_b=cqT_b, wcol=768)
        wv, wb = load_w("w_uqr", l, "col", 4, 0, 768)
        for hh in range(NH_A):
            bA, bbA = ps_short.next()
            mm_acc(bbA, bA[0:64, 0:T], [(wv[:, k, hh * 128:hh * 128 + 64], cqT[:, k, 0:T]) for k in range(4)], [wb] + cqT_b)
            bB, bbB = ps_short.next()
            mm_acc(bbB, bB[0:64, 0:T], [(wv[:, k, hh * 128 + 64:hh * 128 + 128], cqT[:, k, 0:T]) for k in range(4)], [wb] + cqT_b)
            rope_from(bA, bbA, bB, bbB, qr[:, hh, 0:T], qr_b[hh], extra_scale=nm0[0:64, 0:T])

        if stop <= 4:
            return
        sc_a = 192.0 ** -0.5
        for (q, c0, ncs, kp) in segs:
            lk = kp + ncs
            dma("sp", krb[:, 0:lk], SC["kr", l, q][:, 0:lk], reads=[SCB["kr", l, q]], writes=[krb_b])
            nkt = (lk + 127) // 128

            def a_load(hh, q=q, lk=lk):
                return load_kv("akT", "av", l, q, hh, 0, lk, hh * 128)

            def a_units(hh, cur, c0=c0, ncs=ncs, kp=kp, lk=lk, nkt=nkt):
                kb, kbb = cur[0], cur[1]
                us = []
                for kt in range(nkt):
                    nk = min(128, lk - kt * 128)
                    jd = kt * 128 - kp
                    qlo = max(0, jd) if is_p else 0
                    N = ncs - qlo
                    mask = (lambda: [(64, 128, 0, 64)]) if (is_p and jd >= 0) else None
                    us.append(AU(pairs=[(kb[:, kt * 128:kt * 128 + nk], qa[:, hh, c0 + qlo:c0 + ncs]),
                                        (krb[:, kt * 128:kt * 128 + nk], qr[:, hh, c0 + qlo:c0 + ncs])],
                                 reads=[kbb, krb_b, qa_b[hh], qr_b[hh]], nk=nk, N=N, scale=sc_a, mask_fn=mask, acc=0, oc0=qlo, vt=kt,
                                 first=(kt == 0), last=(kt == nkt - 1)))
                return us

            def a_fin(hh, accs, c0=c0, ncs=ncs):
                softmax_fin(accs[0], accs[1], oT[:, hh, c0:c0 + ncs], oT_b[hh], ncs)

            attn_run(NH_A, a_load, a_units, 2, a_fin)

        if stop <= 5:
            return

        if stop <= 6:
            return
        sc_b = 64.0 ** -0.5
        neglam = lams[:, l * 8 + 4:l * 8 + 5]
        subsc = lams[:, l * 8 + 5:l * 8 + 6]
        for (q, c0, ncs, kp) in segs:
            lk = kp + ncs
            nkt = (lk + 127) // 128

            def b_load(hh, q=q, lk=lk):
                return load_kv("dkT", "dv", l, q, hh, 0, lk, hh * 128)

            def b_units(hh, cur, c0=c0, ncs=ncs, kp=kp, lk=lk, nkt=nkt):
                kb, kbb = cur[0], cur[1]
                us = []
                for kt in range(nkt):
                    nk = min(128, lk - kt * 128)
                    jd = kt * 128 - kp
                    qlo = max(0, jd) if is_p else 0
                    N = ncs - qlo

                    def bias_fn(qlo=qlo, N=N, kt=kt, hh=hh, nk=nk):
                        res = []
                        qpos0 = kp + qlo
                        c = 0
                        while c < N:
                            qp = qpos0 + c
                            w = min(128 - (qp % 128), N - c) if is_p else N
                            jj = (qp // 128) - kt
                            qoff = qp % 128
                            if jj <= 1:
                                res.append((c, w, dbias[0:nk, hh, jj, qoff:qoff + w], None))
                            else:
                                res.append((c, N - c, None, dfar[0:nk, hh:hh + 1]))
                                break
                            c += w
                        return res

                    mask = None
                    for cpn in range(2):
                        us.append(AU(pairs=[(kb[64 * cpn:64 * cpn + 64, kt * 128:kt * 128 + nk],
                                             dq_t(hh)[64 * cpn:64 * cpn + 64, hh % 2, c0 + qlo:c0 + ncs])],
                                     reads=[kbb, dq_b(hh)], nk=nk, N=N, bias_fn=bias_fn, scale=sc_b, mask_fn=mask, acc=cpn, oc0=qlo,
                                     vt=kt, first=(kt == 0), last=(kt == nkt - 1)))
                return us

            def b_fin(hh, accs, c0=c0, ncs=ncs):
                ev = []
                for cpn in range(2):
                    ob, obb = accs[cpn]
                    sb_, sbb = accs[2 + cpn]
                    fo, fob = f32r.next()
                    fs, fsb = f32r.next()
                    op("act", lambda h, fo=fo, ob=ob: h.copy(out=fo[:, 0:ncs], in_=ob[:, 0:ncs]), reads=[obb], writes=[fob])
                    op("dve", lambda h, fs=fs, sb_=sb_: h.tensor_copy(out=fs[:, 0:ncs], in_=sb_[:, 0:ncs]), reads=[sbb], writes=[fsb])
                    ev.append((fo, fob, fs, fsb))
                a = []
                for (fo, fob, fs, fsb) in ev:
                    op("dve", lambda h, fs=fs: h.reciprocal(out=fs[:, 0:ncs], in_=fs[:, 0:ncs]), reads=[fsb], writes=[fsb])
                    op("dve", lambda h, fo=fo, fs=fs: h.tensor_tensor(out=fo[:, 0:ncs], in0=fo[:, 0:ncs], in1=fs[:, 0:ncs], op=ALU.mult),
                       reads=[fsb], writes=[fob])
                    a.append((fo, fob))
                (a0, a0b), (a1, a1b) = a
                op("dve", lambda h: h.scalar_tensor_tensor(out=a0[:, 0:ncs], in0=a1[:, 0:ncs], scalar=neglam, in1=a0[:, 0:ncs],
                                                           op0=ALU.mult, op1=ALU.add), reads=[a1b, const_b], writes=[a0b])
                op("act", lambda h: h.activation(out=a1[:, 0:ncs], in_=a0[:, 0:ncs], func=AF.Square), reads=[a0b], writes=[a1b])
                return lambda: b_fin2(hh, a0, a0b, a1, a1b, c0, ncs)

            def b_fin2(hh, a0, a0b, a1, a1b, c0, ncs):
                vbank, vbb_ = ps_short.next()
                op("pe", lambda h: h.matmul(vbank[:, 0:ncs], lhsT=ones128[:, :], rhs=a1[:, 0:ncs], start=True, stop=True),
                   reads=[a1b, const_b], writes=[vbb_], signal=True)
                r2, r2b = f32r.next()
                rstd_from(vbank[:, 0:ncs], vbb_, r2, r2b, 128, RMS_EPS, ncs)
                op("dve", lambda h: h.tensor_tensor(out=a0[:, 0:ncs], in0=a0[:, 0:ncs], in1=r2[:, 0:ncs], op=ALU.mult),
                   reads=[r2b], writes=[a0b])
                op("dve", lambda h: h.tensor_scalar(out=oT[:, 6 + hh, c0:c0 + ncs], in0=a0[:, 0:ncs], scalar1=subsc, scalar2=None,
                                                    op0=ALU.mult), reads=[a0b, const_b], writes=[oT_b[6 + hh]])

            attn_run(NH_B, b_load, b_units, 4, b_fin)

        if stop <= 7:
            return
        gemm_fm("w_in", O_BQ, 768, cons_q)
        want_band_out = (s == 3) or (not is_p)

        def band_out_rows(name):
            return OUT[name][l, :, :]

        rs = RowStage(NT, [(band_out_rows("bk"), None, True)], 768) if want_band_out else None
        gemm_fm("w_in", O_BK, 768, make_cons_k("bkT", rs))
        if rs is not None:
            rs.flush()
        rs = RowStageV(NT, [(band_out_rows("bv"), None, True)] if want_band_out else [], 768, "bv")
        gemm_fm("w_in", O_BV, 768, make_cons_v(rs))
        rs.flush()

        if stop <= 8:
            return
        sc_c = 128.0 ** -0.5
        for (q, c0, ncs, kp) in segs:
            kp = kq("bkT", kp)
            k_lo = max(0, kp - 512) if is_p else 0
            lk = kp + ncs

            bandv = {}

            def c_load(hh, q=q, lk=lk, k_lo=k_lo):
                bbt, bbb = bbr.next()
                dma("sp", bbt[:], bbias_d[l, hh, :, :, :], writes=[bbb])
                kb, kbb = kbr.next()
                n = lk - k_lo
                dma("sp", kb[:, 0:n], SC["bkT", l, q][hh, :, k_lo:lk], reads=[SCB["bkT", l, q]], writes=[kbb])
                if hh % 2 == 0:
                    vb, vbb = vbr.next()
                    src = SC["bv", l, q]
                    nfull = n // 128
                    if nfull:
                        dma("sp", vb[:, 0:2 * nfull, :].rearrange("p (t two) d -> p t (two d)", two=2),
                            src[k_lo:k_lo + nfull * 128, hh * 128:(hh + 2) * 128].rearrange("(t p) c -> p t c", p=128),
                            reads=[SCB["bv", l, q]], writes=[vbb])
                    rem = n - nfull * 128
                    if rem:
                        dma("sp", vb[0:rem, 2 * nfull:2 * nfull + 2, :].rearrange("p two d -> p (two d)"),
                            src[k_lo + nfull * 128:lk, hh * 128:(hh + 2) * 128], reads=[SCB["bv", l, q]], writes=[vbb],
                            append=True)
                    bandv["cur"] = (vb, vbb)
                vb, vbb = bandv["cur"]
                return (kb, kbb, vb, vbb, bbt, bbb)

            def c_units(hh, cur, c0=c0, ncs=ncs, kp=kp, lk=lk, k_lo=k_lo):
                kb, kbb, bbt, bbb = cur[0], cur[1], cur[4], cur[5]
                us = []
                nqt = (ncs + 127) // 128
                for qi in range(nqt):
                    nq = min(128, ncs - qi * 128)
                    qpos = kp + qi * 128
                    kts = [kt for kt in range(qpos // 128 - 4, qpos // 128 + 1) if kt * 128 >= k_lo]
                    for ii, kt in enumerate(kts):
                        j = qpos // 128 - kt
                        nk = min(128, lk - kt * 128)
                        ko = kt * 128 - k_lo

                        def bias_fn(j=j, nk=nk, nq=nq, bbt=bbt):
                            return [(0, nq, bbt[0:nk, j, 0:nq], None)]

                        mask = None
                        us.append(AU(pairs=[(kb[:, ko:ko + nk], qa[:, hh, c0 + qi * 128:c0 + qi * 128 + nq])], reads=[kbb, qa_b[hh]],
                                     nk=nk, N=nq, bias_fn=bias_fn, scale=sc_c, mask_fn=mask, bias_bufs=[bbb], acc=0, oc0=qi * 128,
                                     vt=2 * (ko // 128) + hh % 2, first=(ii == 0), last=(ii == len(kts) - 1)))
                return us

            def c_fin(hh, accs, c0=c0, ncs=ncs):
                softmax_fin(accs[0], accs[1], oT[:, 10 + hh, c0:c0 + ncs], oT_b[10 + hh], ncs)

            attn_run(NH_C, c_load, c_units, 2, c_fin, depth=4)

        if stop <= 9:
            return
        lnq = []
        lnb = {"p": (4, 5)}

        def ln_stats(c):
            lnq.append(c)
            if len(lnq) > 2:
                ln_stats_now(lnq.pop(0))

        def ln_stats_now(c):
            (mb, mbb), (qb, qbb) = banks[lnb["p"][0]], banks[lnb["p"][1]]
            ft, ftb = f32r.next()
            op("act", lambda h: h.activation(out=ft[:, 0:T], in_=xres[:, c, 0:T], func=AF.Square), reads=[xres_b[c]], writes=[ftb])
            op("pe", lambda h: h.matmul(mb[:, 0:T], lhsT=onesD[:, :], rhs=xres[:, c, 0:T], start=(c == 0), stop=(c == 15)),
               reads=[xres_b[c], const_b], writes=[mbb], signal=(c == 15))
            op("pe", lambda h: h.matmul(qb[:, 0:T], lhsT=onesD[:, :], rhs=ft[:, 0:T], start=(c == 0), stop=(c == 15)),
               reads=[ftb, const_b], writes=[qbb], signal=True)

        def ln_finish(gcol, bcol, fp_g, fp_b, want_bf):
            (mb, mbb), (qb, qbb) = banks[lnb["p"][0]], banks[lnb["p"][1]]
            while lnq:
                ln_stats_now(lnq.pop(0))
            op("act", lambda h: h.copy(out=nm0[:, 0:T], in_=mb[:, 0:T]), reads=[mbb], writes=[nm0_b])
            ft, ftb = f32r.next()
            op("dve", lambda h: h.tensor_tensor(out=ft[:, 0:T], in0=nm0[:, 0:T], in1=nm0[:, 0:T], op=ALU.mult), reads=[nm0_b], writes=[ftb])
            op("dve", lambda h: h.tensor_tensor(out=ft[:, 0:T], in0=qb[:, 0:T], in1=ft[:, 0:T], op=ALU.subtract), reads=[qbb],
               writes=[ftb])
            op("act", lambda h: h.activation(out=nm1[:, 0:T], in_=ft[:, 0:T], func=AF.Sqrt, bias=LN_EPS, scale=1.0), reads=[ftb],
               writes=[nm1_b])
            op("dve", lambda h: h.reciprocal(out=nm1[:, 0:T], in_=nm1[:, 0:T]), reads=[nm1_b], writes=[nm1_b])
            for c in range(16):
                ft, ftb = f32r.next()
                op("dve", lambda h, c=c, ft=ft: h.tensor_tensor(out=ft[:, 0:T], in0=xres[:, c, 0:T], in1=nm0[:, 0:T], op=ALU.subtract),
                   reads=[xres_b[c], nm0_b], writes=[ftb])
                op("dve", lambda h, c=c, ft=ft: h.tensor_tensor(out=ft[:, 0:T], in0=ft[:, 0:T], in1=nm1[:, 0:T], op=ALU.mult),
                   reads=[nm1_b], writes=[ftb])
                if want_bf:
                    op("act", lambda h, c=c, ft=ft: h.activation(out=xT[:, c, 0:T], in_=ft[:, 0:T], func=AF.Identity,
                                                               scale=pv(gcol + c), bias=pv(bcol + c)),
                       reads=[ftb, const_b], writes=[xT_b[c]])
                op("pool", lambda h, c=c, ft=ft: h.tensor_scalar(out=xres[:, c, 0:T], in0=ft[:, 0:T], scalar1=fp_g(c), scalar2=fp_b(c),
                                                                op0=ALU.mult, op1=ALU.add), reads=[ftb, const_b], writes=[xres_b[c]])

        def cons_wo(ci, bank, bb, m):
            op("dve", lambda h: h.scalar_tensor_tensor(out=xres[:, ci, 0:T], in0=xres[:, ci, 0:T], scalar=ALPHA, in1=bank[:, 0:T],
                                                       op0=ALU.mult, op1=ALU.add), reads=[bb], writes=[xres_b[ci]])
            lnb["p"] = (4, 5)
            ln_stats(ci)

        gemm_fm("w_o", 0, D, cons_wo, nk=16, rhs=oT, rhs_b=oT_b)
        ln_finish(V_L1G, V_L1B, lambda c: agb[:, l, c:c + 1], lambda c: agb[:, l, 16 + c:17 + c], True)

        if stop <= 10:
            return
        nseg = len(segs)
        L = T // nseg
        ps_dn = ps_long

        def ffn_down(g2, hT, hTb):
            wd, wdb = load_w("w_d", l, "row", g2 * 2, 2, D)
            for c in range(16):
                db, dbb = ps_dn.next()
                mm_acc(dbb, db[:, 0:T], [(wd[:, jj, c * 128:(c + 1) * 128], hT[:, jj, 0:T]) for jj in range(2)], [wdb, hTb])
                op("dve", lambda h, c=c, db=db: h.tensor_tensor(out=xres[:, c, 0:T], in0=db[:, 0:T], in1=xres[:, c, 0:T], op=ALU.add),
                   reads=[dbb], writes=[xres_b[c]])
                if g2 == NFF // 2 - 1:
                    lnb["p"] = (0, 1)
                    ln_stats(c)

        def ffn_mm(j):
            wgu, wgub = load_w("w_gu", l, "col", 16, j * 256, 256)
            gb, gbb = ps_short.next()
            mm_acc(gbb, gb[:, 0:T], [(wgu[:, k, 0:128], xT[:, k, 0:T], [xT_b[k]]) for k in range(16)], [wgub])
            ub, ubb = ps_short.next()
            mm_acc(ubb, ub[:, 0:T], [(wgu[:, k, 128:256], xT[:, k, 0:T], [xT_b[k]]) for k in range(16)], [wgub])
            gp, gpb = gpr.next()
            gpv = gp[:, 0:nseg * (L + 2)].rearrange("p (s c) -> p s c", s=nseg)
            op("act", lambda h: h.copy(out=gpv[:, :, 2:2 + L], in_=gb[:, 0:T].rearrange("p (s c) -> p s c", s=nseg)),
               reads=[gbb], writes=[gpb])
            fu, fub = f32r.next()
            op("act", lambda h: h.copy(out=fu[:, 0:T], in_=ub[:, 0:T]), reads=[ubb], writes=[fub])
            op("pool", lambda h: h.tensor_copy(out=gpv[:, :, 0:2], in_=carry[:, l, 0:nseg, :, j]), reads=[carry_b[l]], writes=[gpb])
            op("pool", lambda h: h.tensor_copy(out=carry[:, l, 0:nseg, :, j], in_=gpv[:, :, L:L + 2]), reads=[gpb],
               writes=[carry_b[l]])
            return (j, gpv, gpb, fu, fub)

        def ffn_post(st_):
            j, gpv, gpb, fu, fub = st_
            hT, hTb = hTr.items[(j // 2) % 2]
            jj = j % 2
            f1, f1b = f32r.next()
            f1v = f1[:, 0:T].rearrange("p (s c) -> p s c", s=nseg)
            cw = lambda i: pvec[:, l, V_CW + j * 3 + i:V_CW + j * 3 + i + 1]
            op("act", lambda h: h.activation(out=f1v, in_=gpv[:, :, 2:2 + L], func=AF.Identity, scale=cw(2),
                                             bias=pvec[:, l, V_CB + j:V_CB + j + 1]), reads=[gpb, const_b], writes=[f1b])
            op("dve", lambda h: h.scalar_tensor_tensor(out=f1v, in0=gpv[:, :, 1:1 + L], scalar=cw(1), in1=f1v, op0=ALU.mult,
                                                       op1=ALU.add), reads=[gpb, const_b], writes=[f1b])
            op("dve", lambda h: h.scalar_tensor_tensor(out=f1v, in0=gpv[:, :, 0:L], scalar=cw(0), in1=f1v, op0=ALU.mult,
                                                       op1=ALU.add), reads=[gpb, const_b], writes=[f1b])
            op("act", lambda h: h.activation(out=f1[:, 0:T], in_=f1[:, 0:T], func=AF.Silu), reads=[f1b], writes=[f1b])
            op("dve", lambda h: h.tensor_tensor(out=hT[:, jj, 0:T], in0=fu[:, 0:T], in1=f1[:, 0:T], op=ALU.mult), reads=[fub, f1b],
               writes=[hTb])

        stq = {}
        for j in range(NFF):
            stq[j] = ffn_mm(j)
            if j % 2 == 1 and j >= 3:
                g_ = (j - 3) // 2
                ffn_down(g_, *hTr.items[g_ % 2])
            if j >= 1:
                ffn_post(stq.pop(j - 1))
        ffn_post(stq.pop(NFF - 1))
        ffn_down(NFF // 2 - 1, *hTr.items[(NFF // 2 - 1) % 2])

        if stop <= 11:
            return
        last_layer = (l == nlayers - 1)
        ln_finish(V_L2G, V_L2B, lambda c: pv(V_L2G + c), lambda c: pv(V_L2B + c), not last_layer)

        if s == 3 or not is_p:
            for si, (q, c0, ncs, kp) in enumerate(segs):
                bank, bb = ps_short.next()
                for t_ in range(2):
                    transpose_to(bank, bb, bank[0:NFF, t_ * 128:(t_ + 1) * 128], carry[:, l, si, t_, :], 128, [carry_b[l]],
                                 signal=(t_ == 1))
                op("act", lambda h, si=si, bank=bank: h.copy(out=cvst[:, si, :, :],
                                                            in_=bank[0:NFF, 0:256].rearrange("p (t c) -> p t c", t=2)),
                   reads=[bb], writes=[cvst_b])
                dst = p_conv[l] if is_p else s_conv[l, si]
                dma("pool", dst.rearrange("t (j p) -> j t p", p=128), cvst[:, si, :, :], reads=[cvst_b], store=True)


    def prefetched(items, load, process, depth=5):
        q_ = []
        nxt_i = 0
        for i_, it in enumerate(items):
            while nxt_i < len(items) and nxt_i <= i_ + depth:
                q_.append(load(items[nxt_i]))
                nxt_i += 1
            process(it, q_.pop(0))

    def prep_caches():
        T = 512
        pslots = []
        for k_ in range(8):
            pb_ = Buf()
            for c_ in (2 * k_, 2 * k_ + 1):
                pb_.w.extend(xres_b[c_].w)
                pb_.r.extend(xres_b[c_].r)
            pslots.append((xres[:, 2 * k_:2 * k_ + 2, :].rearrange("p a b -> p (a b)"), pb_))
        stgr = Ring(pslots)
        vst_pb = [Buf() for _ in range(8)]
        for b_ in vst_pb:
            b_.w.extend(vst_b.w)
            b_.r.extend(vst_b.r)
        for l in range(nlayers):
            for si in range(2):
                q = 1 + si

                def a_ld(it):
                    blk, tt = it
                    r0 = blk * 512 + tt * 128
                    st, stb = stgr.next()
                    dma("sp", st[:, 0:256], c_ckv[l, si, r0:r0 + 128, :], writes=[stb])
                    dma("sp", st[:, 256:320], c_kr[l, si, r0:r0 + 128, :], writes=[stb], append=True)
                    return st, stb

                def a_pr(it, cur):
                    blk, tt = it
                    st, stb = cur
                    bank, bb = ps_short.next()
                    for ci in range(2):
                        transpose_to(bank, bb, bank[:, ci * 128:(ci + 1) * 128], st[:, ci * 128:(ci + 1) * 128], 128, [stb],
                                     signal=False)
                    transpose_to(bank, bb, bank[0:64, 256:384], st[:, 256:320], 128, [stb], signal=True)
                    op("act", lambda h: h.copy(out=ckvT[:, :, tt * 128:(tt + 1) * 128],
                                               in_=bank[:, 0:256].rearrange("p (c t) -> p c t", c=2)), reads=[bb], writes=ckvT_b)
                    op("dve", lambda h: h.tensor_copy(out=qr[:, 0, tt * 128:(tt + 1) * 128], in_=bank[0:64, 256:384]),
                       reads=[bb], writes=[qr_b[0]])
                    if tt < 3:
                        return
                    dma("pool", SC["kr", l, q][:, blk * 512:(blk + 1) * 512], qr[:, 0, 0:512], reads=[qr_b[0]],
                        writes=[SCB["kr", l, q]], append=True)
                    wv, wb = load_w("w_ukk", l, "col", 2, 0, 768)
                    for hh in range(NH_A):
                        bank, bb = ps_short.next()
                        mm_acc(bb, bank[:, 0:T], [(wv[:, k, hh * 128:(hh + 1) * 128], ckvT[:, k, 0:T]) for k in range(2)], [wb] + ckvT_b)
                        pt, ptb = ptr.next()
                        op("act", lambda h, pt=pt, bank=bank: h.copy(out=pt[:, 0:T], in_=bank[:, 0:T]), reads=[bb], writes=[ptb])
                        dma("pool", SC["akT", l, q][hh, :, blk * 512:(blk + 1) * 512], pt[:, 0:T], reads=[ptb],
                            writes=[SCB["akT", l, q]], append=True)
                    wv, wb = load_w("w_ukv", l, "col", 2, 0, 768)
                    for t2 in range(4):
                        for (cc0, cw) in ((0, 512), (512, 256)):
                            bank, bb = ps_short.next()
                            mm_acc(bb, bank[:, 0:cw], [(ckvT[:, k, t2 * 128:(t2 + 1) * 128], wv[:, k, cc0:cc0 + cw]) for k in range(2)],
                                   [wb] + ckvT_b)
                            vb_ = vst_pb[t2 * 2 + (1 if cc0 else 0)]
                            op("act" if cc0 else "dve", lambda h, bank=bank, cc0=cc0, cw=cw, t2=t2: (
                                h.copy(out=vst[:, t2, cc0:cc0 + cw], in_=bank[:, 0:cw]) if cc0 else
                                h.tensor_copy(out=vst[:, t2, cc0:cc0 + cw], in_=bank[:, 0:cw])), reads=[bb], writes=[vb_])
                    dma("pool", SC["av", l, q][blk * 512:(blk + 1) * 512, :].rearrange("(t p) c -> p t c", p=128), vst[:, 0:4, :],
                        reads=vst_pb, writes=[SCB["av", l, q]], append=True)

                prefetched([(blk, tt) for blk in range(PAST // 512) for tt in range(4)], a_ld, a_pr)

                def k_ld(it):
                    kind, blk, hh = it
                    st, stb = stgr.next()
                    sv = st[:, 0:512].rearrange("p (t c) -> p t c", t=4)
                    src = c_dk if kind == "dkT" else c_bk
                    dma("sp", sv, src[l, si, blk * 512:(blk + 1) * 512, hh * 128:(hh + 1) * 128].rearrange("(t p) c -> p t c", p=128),
                        writes=[stb])
                    return sv, stb

                def k_pr(it, cur):
                    kind, blk, hh = it
                    sv, stb = cur
                    bank, bb = ps_short.next()
                    for tt in range(4):
                        transpose_to(bank, bb, bank[:, tt * 128:(tt + 1) * 128], sv[:, tt, :], 128, [stb], signal=(tt == 3))
                    pt, ptb = ptr.next()
                    op("act" if hh % 2 else "dve", lambda h: (
                        h.copy(out=pt[:, 0:512], in_=bank[:, 0:512]) if hh % 2 else h.tensor_copy(out=pt[:, 0:512], in_=bank[:, 0:512])),
                       reads=[bb], writes=[ptb])
                    dma("pool", SC[kind, l, q][hh, :, blk * 512:(blk + 1) * 512], pt[:, 0:512], reads=[ptb],
                        writes=[SCB[kind, l, q]], append=True)

                for r0 in range(0, PAST, 1024):
                    dma("pool", SC["dv", l, q][r0:r0 + 1024, :], c_dv[l, si, r0:r0 + 1024, :], writes=[SCB["dv", l, q]], append=True)
                dma("pool", SC["bv", l, q][0:512, :], c_bv[l, si, :, :], writes=[SCB["bv", l, q]], append=True)
                prefetched([("dkT", blk, hh) for blk in range(PAST // 512) for hh in range(NH_B)] +
                           [("bkT", 0, hh) for hh in range(NH_C)], k_ld, k_pr)
                dma("sp", cvst[:, si, :, :], c_conv[l, si].rearrange("t (j p) -> j t p", p=128), writes=[cvst_b])
                bank, bb = ps_short.next()
                for t_ in range(2):
                    transpose_to(bank, bb, bank[:, t_ * NFF:(t_ + 1) * NFF], cvst[:, si, t_, :], NFF, [cvst_b], signal=(t_ == 1))
                op("act", lambda h, bank=bank, si=si, l=l: h.copy(out=carry[:, l, si, :, :],
                                                                 in_=bank[:, 0:2 * NFF].rearrange("p (t j) -> p t j", t=2)),
                   reads=[bb], writes=[carry_b[l]])
        for b_ in vst_pb:
            vst_b.r = list(vst_b.r) + list(b_.r)
            vst_b.w = list(vst_b.w) + list(b_.w)
        for k_, (_, pb_) in enumerate(pslots):
            for c_ in (2 * k_, 2 * k_ + 1):
                xres_b[c_].r = list(xres_b[c_].r) + list(pb_.r)
                xres_b[c_].w = list(xres_b[c_].w) + list(pb_.w)

    for s in supers:
        is_p = s < 4
        T = 512 if is_p else 128
        NT = T // 128
        xin = x_p[s * 512:(s + 1) * 512, :] if is_p else x_s
        if not is_p:
            prep_caches()
        for tt in ([] if "nox" in DBG else range(NT)):
            for hf in range(2):
                st, stb = stgr.next()
                dma("sp", st[:, 0:1024], xin[tt * 128:(tt + 1) * 128, hf * 1024:(hf + 1) * 1024], writes=[stb])
                for g4 in range(2):
                    bank, bb = ps_short.next()
                    for i in range(4):
                        cc = g4 * 4 + i
                        transpose_to(bank, bb, bank[:, i * 128:(i + 1) * 128], st[:, cc * 128:(cc + 1) * 128], 128, [stb],
                                     signal=(i == 3))
                    c0 = hf * 8 + g4 * 4
                    if "x_nodve" not in DBG:
                        op("dve", lambda h, bank=bank, c0=c0, tt=tt: h.tensor_copy(
                            out=xres[:, c0:c0 + 4, tt * 128:(tt + 1) * 128], in_=bank[:, 0:512].rearrange("p (c t) -> p c t", c=4)),
                           reads=[bb], writes=xres_b[c0:c0 + 4])
                    if "x_noact" not in DBG:
                        op("act", lambda h, bank=bank, c0=c0, tt=tt: h.copy(
                            out=xT[:, c0:c0 + 4, tt * 128:(tt + 1) * 128], in_=bank[:, 0:512].rearrange("p (c t) -> p c t", c=4)),
                           reads=[bb], writes=xT_b[c0:c0 + 4])
        if is_p:
            segs = [(0, 0, 512, s * 512)]
            pos0 = s * 512
        else:
            segs = [(1, 0, 64, PAST), (2, 64, 64, PAST)]
            pos0 = None
        for l in range(nlayers):
            layer(s, l, T, segs, pos0 if is_p else PAST)
        yout = y_p[s * 512:(s + 1) * 512, :] if is_p else y_s
        for tt in ([] if "noy" in DBG else range(NT)):
            for hf in range(2):
                st, stb = stgr.next()
                for g4 in range(2):
                    bank, bb = ps_short.next()
                    for i in range(4):
                        cc = hf * 8 + g4 * 4 + i
                        transpose_to(bank, bb, bank[:, i * 128:(i + 1) * 128], xres[:, cc, tt * 128:(tt + 1) * 128], 128,
                                     [xres_b[cc]], signal=(i == 3))
                    op("act" if g4 else "dve", lambda h, bank=bank, st=st, g4=g4: (
                        h.copy(out=st[:, g4 * 512:(g4 + 1) * 512], in_=bank[:, 0:512]) if g4 else
                        h.tensor_copy(out=st[:, g4 * 512:(g4 + 1) * 512], in_=bank[:, 0:512])), reads=[bb], writes=[stb])
                dma("pool", yout[tt * 128:(tt + 1) * 128, hf * 1024:(hf + 1) * 1024], st[:, 0:1024], reads=[stb], store=True)

    if not plan:
        E = C.E["pool"]
        for tok in C.store_toks:
            E.wait(tok)
        for qn in ("pool", "sp"):
            q = C.Q[qn]
            if q.sems is not None:
                for k in range(q.k):
                    n = (q.i - 1 - k) // q.k + 1 if q.i > k else 0
                    if n > 0:
                        sid, sem = q.sems[k]
                        E.wait(Tok(sid, sem, 16 * n, None))
    return nc, C


def _t5_bucket(rel):
    half, exact = 16, 8
    n = np.abs(rel)
    nf = np.maximum(n, 1).astype(np.float32)
    large = exact + (np.log(nf / exact) / np.float32(math.log(128 / exact)) * (half - exact)).astype(np.int32)
    large = np.minimum(large, half - 1)
    return np.where(rel > 0, half, 0) + np.where(n < exact, n, large)


_CACHE = {}


def _host_prep(x_prompt, x_sample, cache_mla_ckv, cache_mla_krope, cache_diff_k, cache_diff_v, cache_band_k, cache_band_v,
           state_ffn_conv, t5_table, w_in, mla_q_norm, mla_w_uq, mla_kv_norm, mla_w_ukv, diff_lq1, diff_lk1, diff_lq2,
           diff_lk2, diff_subln, band_rel_table, w_o, ln1_g, ln1_b, ffn_w_gate, ffn_w_up, ffn_conv_w, ffn_conv_b,
           ffn_w_down, ln2_g, ln2_b):
    f = lambda a: np.ascontiguousarray(np.asarray(a, dtype=np.float32))
    x_prompt, x_sample = f(x_prompt), f(x_sample)
    w_in = f(w_in)
    mla_w_uq = f(mla_w_uq)
    mla_w_ukv = f(mla_w_ukv)
    w_kr = np.concatenate([w_in[:, :, 768:832], w_in[:, :, 800:832], w_in[:, :, 768:800]], axis=2)
    uq = mla_w_uq.reshape(DEPTH, 512, NH_A, 192)
    w_uqn = uq[:, :, :, 0:128].reshape(DEPTH, 512, 768)
    w_uqr = np.concatenate([uq[:, :, :, 128:192], uq[:, :, :, 160:192], uq[:, :, :, 128:160]], axis=3).reshape(DEPTH, 512, 768)
    ukv = mla_w_ukv.reshape(DEPTH, 256, NH_A, 256)
    w_ukk = ukv[:, :, :, 0:128].reshape(DEPTH, 256, 768)
    w_ukv = ukv[:, :, :, 128:256].reshape(DEPTH, 256, 768)
    w_gu = np.ascontiguousarray(np.stack([f(ffn_w_gate).reshape(DEPTH, D, NFF, 128), f(ffn_w_up).reshape(DEPTH, D, NFF, 128)],
                                         axis=3).reshape(DEPTH, D, 2 * DFF))
    pvec = np.zeros((128, DEPTH, V_END), np.float32)
    cm = lambda v: f(v).reshape(-1, 128).T
    for l in range(DEPTH):
        pvec[:, l, V_QG:V_QG + 4] = cm(mla_q_norm[l])
        pvec[:, l, V_KVG:V_KVG + 2] = cm(mla_kv_norm[l])
        pvec[:, l, V_SUB:V_SUB + 1] = cm(diff_subln[l])
        pvec[:, l, V_L1G:V_L1G + 16] = cm(ln1_g[l])
        pvec[:, l, V_L1B:V_L1B + 16] = cm(ln1_b[l])
        pvec[:, l, V_L2G:V_L2G + 16] = cm(ln2_g[l])
        pvec[:, l, V_L2B:V_L2B + 16] = cm(ln2_b[l])
        cw = f(ffn_conv_w[l])
        pvec[:, l, V_CW:V_CW + 132] = cw.reshape(3, NFF, 128).transpose(2, 1, 0).reshape(128, 132)
        pvec[:, l, V_CB:V_CB + 44] = cm(ffn_conv_b[l])
    lamv = np.zeros((128, DEPTH, 4, 64), np.float32)
    for l in range(DEPTH):
        for i, v in enumerate((diff_lq1, diff_lk1, diff_lq2, diff_lk2)):
            lamv[:, l, i, :] = f(v)[l][None, :]
    half = 32
    inv = (10000.0 ** (-np.arange(half, dtype=np.float32) / half)).astype(np.float32)
    pos = np.arange(SEQ + DSEQ, dtype=np.float32)
    ang = pos[:, None] * inv[None, :]
    cosT = np.concatenate([np.cos(ang), np.cos(ang)], 1).T.astype(np.float32)
    sinT = np.concatenate([-np.sin(ang), np.sin(ang)], 1).T.astype(np.float32)
    t5 = f(t5_table)
    kp = np.arange(128)[:, None]
    qq = np.arange(128)[None, :]
    dbias = np.zeros((128, NH_B, 2, 128), np.float32)
    for j in range(2):
        rel = (kp - qq) - 128 * j
        dbias[:, :, j, :] = t5[_t5_bucket(rel)].transpose(0, 2, 1)
    NEG = np.float32(-30000.0)
    dbias[64:128, :, 0, 0:64] = NEG
    dfar = np.broadcast_to(t5[15][None, :], (128, NH_B)).copy()
    brt = f(band_rel_table)
    bbias = np.zeros((DEPTH, NH_C, 128, 5, 128), np.float32)
    for j in range(5):
        idx = np.clip(128 * j + qq - kp, -256, 256) + 256
        bbias[:, :, :, j, :] = brt[:, :, idx]
    bbias[:, :, 64:128, 0, 0:64] = np.float32(-30000.0)
    bbias[:, :, 0:64, 4, 64:128] = np.float32(-30000.0)
    def tiles_col(W, starts, width):
        L_, K_, _ = W.shape
        nk_ = K_ // 128
        out = np.empty((L_, len(starts), 128, nk_ * width), np.float32)
        Wr = W.reshape(L_, nk_, 128, -1)
        for ti, st_ in enumerate(starts):
            out[:, ti] = Wr[:, :, :, st_:st_ + width].transpose(0, 2, 1, 3).reshape(L_, 128, nk_ * width)
        return out

    w_d_t = f(ffn_w_down).reshape(DEPTH, 22, 2, 128, D).transpose(0, 1, 3, 2, 4).reshape(DEPTH, 22, 128, 2 * D)
    shared = {
        "w_in": tiles_col(w_in, IN_STARTS, 256), "w_kr": tiles_col(f(w_kr), [0], 128), "w_uqn": tiles_col(f(w_uqn), [0], 768),
        "w_uqr": tiles_col(f(w_uqr), [0], 768), "w_ukk": tiles_col(f(w_ukk), [0], 768), "w_ukv": tiles_col(f(w_ukv), [0], 768),
        "w_o": tiles_col(f(w_o), [256 * i for i in range(8)], 256), "w_gu": tiles_col(w_gu, [256 * i for i in range(44)], 256),
        "w_d": np.ascontiguousarray(w_d_t),
        "pvec": pvec, "lamv": lamv, "cosT": f(cosT), "sinT": f(sinT), "dbias": dbias, "dfar": dfar, "bbias": bbias,
    }
    ckv, ckr = f(cache_mla_ckv), f(cache_mla_krope)
    cdk = f(cache_diff_k).reshape(DEPTH, 16, PAST, 512)
    cdv = f(cache_diff_v).reshape(DEPTH, 16, PAST, 512)
    cbk = f(cache_band_k).reshape(DEPTH, 16, 512, 768)
    cbv = f(cache_band_v).reshape(DEPTH, 16, 512, 768)
    ccv = f(state_ffn_conv)
    in_maps = []
    for i in range(8):
        m = dict(shared)
        m["x_p"] = x_prompt[i]
        m["x_s"] = np.ascontiguousarray(x_sample[2 * i:2 * i + 2].reshape(128, D))
        sl = slice(2 * i, 2 * i + 2)
        m["c_ckv"] = np.ascontiguousarray(ckv[:, sl])
        m["c_kr"] = np.ascontiguousarray(ckr[:, sl])
        m["c_dk"] = np.ascontiguousarray(cdk[:, sl])
        m["c_dv"] = np.ascontiguousarray(cdv[:, sl])
        m["c_bk"] = np.ascontiguousarray(cbk[:, sl])
        m["c_bv"] = np.ascontiguousarray(cbv[:, sl])
        m["c_conv"] = np.ascontiguousarray(ccv[:, sl])
        in_maps.append(m)
    return in_maps


def _assemble(R):
    st = lambda k: np.stack([np.asarray(r[k]) for r in R], 0)
    y_prompt = st("y_p")
    y_sample = st("y_s").reshape(16, DSEQ, D)
    pm = lambda k, shp: np.ascontiguousarray(np.moveaxis(st(k), 0, 1)).reshape(shp)
    p_ckv = pm("p_ckv", (DEPTH, 8, SEQ, 256))
    p_kr = pm("p_krope", (DEPTH, 8, SEQ, 64))
    p_dk = pm("p_dk", (DEPTH, 8, SEQ, NH_B, 128))
    p_dv = pm("p_dv", (DEPTH, 8, SEQ, NH_B, 128))
    p_bk = pm("p_bk", (DEPTH, 8, 512, NH_C, 128))
    p_bv = pm("p_bv", (DEPTH, 8, 512, NH_C, 128))
    p_cv = pm("p_conv", (DEPTH, 8, 2, DFF))
    s_ckv = pm("s_ckv", (DEPTH, 16, DSEQ, 256))
    s_kr = pm("s_krope", (DEPTH, 16, DSEQ, 64))
    s_dk = pm("s_dk", (DEPTH, 16, DSEQ, NH_B, 128))
    s_dv = pm("s_dv", (DEPTH, 16, DSEQ, NH_B, 128))
    s_bk = pm("s_bk", (DEPTH, 16, DSEQ, NH_C, 128))
    s_bv = pm("s_bv", (DEPTH, 16, DSEQ, NH_C, 128))
    s_cv = pm("s_conv", (DEPTH, 16, 2, DFF))
    return (y_prompt, y_sample, p_ckv, p_kr, p_dk, p_dv, p_bk, p_bv, p_cv, s_ckv, s_kr, s_dk, s_dv, s_bk, s_bv, s_cv)


def kernel(**inputs):
    in_maps = _host_prep(**inputs)
    if "nc" not in _CACHE:
        _, C0 = build(True, None, SUPERS, NLAYERS)
        nc, C1 = build(False, C0.wrec, SUPERS, NLAYERS)
        _CACHE["nc"] = nc
    nc = _CACHE["nc"]
    res = run_bass_kernel_spmd(nc, in_maps, core_ids=list(range(8)))
    return _assemble(res.results)
```
